# Optimizing a Trainium2 kernel written in Bass

```python
import math
import jax, jax.numpy as jnp
from jax import lax
import numpy as np

D_MODEL = 2048
BATCH = 16
SEQ = 2048
DEPTH = 2
DEC_BATCH = 4
DEC_SEQ = 8192
PAST_LEN = 128

N_HEADS = 8
HEAD_DIM = 128
V_DIM = 2 * HEAD_DIM
QK_WIDTH = N_HEADS * 2 * HEAD_DIM
ATTN_WIDTH = N_HEADS * V_DIM
Q_BLOCK = 128
N_FGROUPS = 8
FGROUP_DIM = 128
FNET_WIDTH = N_FGROUPS * FGROUP_DIM
IN_WIDTH = 2 * QK_WIDTH + ATTN_WIDTH + FNET_WIDTH + 2 * D_MODEL
D_FF = ((8 * D_MODEL + 3 * 256 - 1) // (3 * 256)) * 256
N_BUCKETS = 32
MAX_DISTANCE = 128
ALPHA = (2.0 * DEPTH) ** 0.25
BETA = (8.0 * DEPTH) ** -0.25
LN_EPS = 1e-5

kernel_name = "hybrid_diffattn_fnet_gated_encoder"


def _layernorm(x, g, b):
    xf = x.astype(jnp.float32)
    mu = jnp.mean(xf, axis=-1, keepdims=True)
    var = jnp.mean(jnp.square(xf - mu), axis=-1, keepdims=True)
    return ((xf - mu) * lax.rsqrt(var + LN_EPS) * g.astype(jnp.float32) + b.astype(jnp.float32)).astype(x.dtype)


def _rel_bucket(rel):
    nb = N_BUCKETS // 2
    ret = jnp.where(rel > 0, nb, 0)
    n = jnp.abs(rel)
    max_exact = nb // 2
    nf = jnp.maximum(n, 1).astype(jnp.float32)
    large = max_exact + (jnp.log(nf / max_exact) / math.log(MAX_DISTANCE / max_exact) * (nb - max_exact)).astype(jnp.int32)
    large = jnp.minimum(large, nb - 1)
    return ret + jnp.where(n < max_exact, n, large)


def _diff_attention(q, k, v, lam, rel_bias, lam_init):
    B, S = q.shape[0], q.shape[1]
    nblk = S // Q_BLOCK
    k1 = k[..., 0, :]
    k2 = k[..., 1, :]
    qb = (q * (HEAD_DIM ** -0.5)).reshape(B, nblk, Q_BLOCK, N_HEADS, 2, HEAD_DIM).transpose(1, 0, 3, 2, 4, 5)
    starts = jnp.arange(nblk, dtype=jnp.int32) * Q_BLOCK
    k_pos = jnp.arange(S, dtype=jnp.int32)
    lamf = lam.astype(jnp.float32)
    lam_full = jnp.exp(jnp.sum(lamf[0] * lamf[1])) - jnp.exp(jnp.sum(lamf[2] * lamf[3])) + lam_init

    def block(args):
        qblk, start = args
        q_pos = start + jnp.arange(Q_BLOCK, dtype=jnp.int32)
        bias = rel_bias[_rel_bucket(k_pos[None, :] - q_pos[:, None])]
        bias = bias.transpose(2, 0, 1).astype(jnp.float32)[None]
        s1 = jnp.einsum('bhqd,bshd->bhqs', qblk[..., 0, :], k1).astype(jnp.float32) + bias
        s2 = jnp.einsum('bhqd,bshd->bhqs', qblk[..., 1, :], k2).astype(jnp.float32) + bias
        a = jax.nn.softmax(s1, axis=-1) - lam_full * jax.nn.softmax(s2, axis=-1)
        return jnp.einsum('bhqs,bshd->bqhd', a.astype(v.dtype), v)

    o = lax.map(block, (qb, starts))
    return o.transpose(1, 0, 2, 3, 4).reshape(B, S, N_HEADS, V_DIM)


def _fourier_mix(f):
    B, S = f.shape[0], f.shape[1]
    fg = f.reshape(B, S, N_FGROUPS, FGROUP_DIM).astype(jnp.float32)
    y = jnp.fft.fft2(fg, axes=(1, 3), norm="ortho").real
    return y.reshape(B, S, FNET_WIDTH).astype(f.dtype)


def _layer(x, l, rel_bias, w_in, b_gate, lam, subln_g, w_br_attn, w_br_fnet, w_out,
           ln1_g, ln1_b, w_gu, w_down, ln2_g, ln2_b):
    B, S = x.shape[0], x.shape[1]
    lam_init = 0.8 - 0.6 * math.exp(-0.3 * l)
    h = x @ w_in[l]
    o0 = QK_WIDTH
    o1 = o0 + QK_WIDTH
    o2 = o1 + ATTN_WIDTH
    o3 = o2 + FNET_WIDTH
    q = h[..., :o0].reshape(B, S, N_HEADS, 2, HEAD_DIM)
    k = h[..., o0:o1].reshape(B, S, N_HEADS, 2, HEAD_DIM)
    v = h[..., o1:o2].reshape(B, S, N_HEADS, V_DIM)
    f = h[..., o2:o3]
    g = h[..., o3:] + b_gate[l]

    o = _diff_attention(q, k, v, lam[l], rel_bias, lam_init)
    of = o.astype(jnp.float32)
    of = of * lax.rsqrt(jnp.mean(jnp.square(of), axis=-1, keepdims=True) + LN_EPS)
    o = (of * subln_g[l].astype(jnp.float32) * (1.0 - lam_init)).astype(x.dtype)
    y_attn = o.reshape(B, S, ATTN_WIDTH) @ w_br_attn[l]

    y_fnet = _fourier_mix(f) @ w_br_fnet[l]

    gates = jax.nn.sigmoid(g)
    merged = gates[..., :D_MODEL] * y_attn + gates[..., D_MODEL:] * y_fnet
    x = _layernorm(ALPHA * x + merged @ w_out[l], ln1_g[l], ln1_b[l])

    gu = x @ w_gu[l]
    y_ffn = (jax.nn.silu(gu[..., :D_FF]) * gu[..., D_FF:]) @ w_down[l]
    return _layernorm(ALPHA * x + y_ffn, ln2_g[l], ln2_b[l])


def _trunk(x, rel_bias, ln_in_g, ln_in_b, w_in, b_gate, lam, subln_g, w_br_attn, w_br_fnet,
           w_out, ln1_g, ln1_b, w_gu, w_down, ln2_g, ln2_b):
    x = _layernorm(x, ln_in_g, ln_in_b)
    for l in range(DEPTH):
        x = _layer(x, l, rel_bias, w_in, b_gate, lam, subln_g, w_br_attn, w_br_fnet, w_out,
                   ln1_g, ln1_b, w_gu, w_down, ln2_g, ln2_b)
    return x


def setup_inputs(seed: int = 0) -> dict:
    key = jax.random.key(seed)
    ks = jax.random.split(key, 20)
    f32 = jnp.float32

    def nrm(k, shape, scale):
        return jax.random.normal(k, shape, f32) * scale

    return {
        "x_prompt": nrm(ks[0], (BATCH, SEQ, D_MODEL), 1.0),
        "x_sample": nrm(ks[1], (DEC_BATCH, DEC_SEQ, D_MODEL), 1.0),
        "rel_bias": nrm(ks[2], (N_BUCKETS, N_HEADS), 0.5),
        "ln_in_g": 1.0 + nrm(ks[3], (D_MODEL,), 0.02),
        "ln_in_b": nrm(ks[4], (D_MODEL,), 0.02),
        "w_in": nrm(ks[5], (DEPTH, D_MODEL, IN_WIDTH), D_MODEL ** -0.5),
        "b_gate": nrm(ks[6], (DEPTH, 2 * D_MODEL), 0.02),
        "lam": nrm(ks[7], (DEPTH, 4, HEAD_DIM), 0.1),
        "subln_g": 1.0 + nrm(ks[8], (DEPTH, V_DIM), 0.02),
        "w_br_attn": nrm(ks[9], (DEPTH, ATTN_WIDTH, D_MODEL), ATTN_WIDTH ** -0.5),
        "w_br_fnet": nrm(ks[10], (DEPTH, FNET_WIDTH, D_MODEL), FNET_WIDTH ** -0.5),
        "w_out": nrm(ks[11], (DEPTH, D_MODEL, D_MODEL), BETA * D_MODEL ** -0.5),
        "ln1_g": 1.0 + nrm(ks[12], (DEPTH, D_MODEL), 0.02),
        "ln1_b": nrm(ks[13], (DEPTH, D_MODEL), 0.02),
        "w_gu": nrm(ks[14], (DEPTH, D_MODEL, 2 * D_FF), D_MODEL ** -0.5),
        "w_down": nrm(ks[15], (DEPTH, D_FF, D_MODEL), BETA * D_FF ** -0.5),
        "ln2_g": 1.0 + nrm(ks[16], (DEPTH, D_MODEL), 0.02),
        "ln2_b": nrm(ks[17], (DEPTH, D_MODEL), 0.02),
    }


def reference(x_prompt, x_sample, rel_bias, ln_in_g, ln_in_b, w_in, b_gate, lam, subln_g,
              w_br_attn, w_br_fnet, w_out, ln1_g, ln1_b, w_gu, w_down, ln2_g, ln2_b):
    y_prompt = _trunk(x_prompt, rel_bias, ln_in_g, ln_in_b, w_in, b_gate, lam, subln_g,
                      w_br_attn, w_br_fnet, w_out, ln1_g, ln1_b, w_gu, w_down, ln2_g, ln2_b)
    y_sample = _trunk(x_sample, rel_bias, ln_in_g, ln_in_b, w_in, b_gate, lam, subln_g,
                      w_br_attn, w_br_fnet, w_out, ln1_g, ln1_b, w_gu, w_down, ln2_g, ln2_b)
    return (y_prompt, y_sample)
```

```python
import math
from contextlib import ExitStack

import numpy as np
import ml_dtypes

import concourse.bass as bass
import concourse.mybir as mybir
from concourse.bass_utils import run_bass_kernel_spmd

F32 = mybir.dt.float32
BF16 = mybir.dt.bfloat16
AF = mybir.ActivationFunctionType
ALU = mybir.AluOpType
AX = mybir.AxisListType

D = 2048
NH = 8
HD = 128
VD = 256
NG = 8
GD = 128
FW = 1024
INW = 11264
DFF = 5632
DEPTH = 2
ALPHA = (2.0 * DEPTH) ** 0.25
EPS = 1e-5
MASKV = -30000.0
OFF_Q, OFF_K, OFF_V, OFF_F, OFF_G = 0, 2048, 4096, 6144, 7168
TABW = 1280


class Buf:
    __slots__ = ("name", "w", "r", "persist")
    ALL = []

    def __init__(self, name, persist=False):
        self.name = name
        self.w = {}
        self.r = {}
        self.persist = persist
        Buf.ALL.append(self)

    @staticmethod
    def reset_all():
        for b in Buf.ALL:
            if not b.persist:
                b.w = {}
                b.r = {}


class DBuf(Buf):
    __slots__ = ()


class Sched:
    CE = ("pe", "act", "dve", "pool")

    def __init__(self, nc, stack):
        self.nc = nc
        self.stack = stack
        self.eng = dict(pe=nc.tensor, act=nc.scalar, dve=nc.vector, pool=nc.gpsimd, sp=nc.sync)
        self.stream = {k: [] for k in self.eng}
        self.done = {k: stack.enter_context(nc.semaphore("done_" + k)) for k in self.CE}
        self.cnt = {self.done[k]: 0 for k in self.CE}
        self.seen = {k: {} for k in self.eng}
        self.nsem = 0
        self.pool = []
        self.pool_i = 0
        self.nobarrier = set()
        self.in_loop = False

    def new_sem(self, name, pooled=True):
        if pooled:
            if self.pool_i < len(self.pool):
                s = self.pool[self.pool_i]
                self.pool_i += 1
                return s
            name = f"pool{len(self.pool)}"
        s = self.stack.enter_context(self.nc.semaphore(name))
        self.cnt[s] = 0
        self.nsem += 1
        if pooled:
            self.pool.append(s)
            self.pool_i += 1
        return s

    def _deps(self, e, reads, writes):
        need = {}
        for b in reads:
            for s, v in b.w.items():
                if need.get(s, 0) < v:
                    need[s] = v
        for b in writes:
            for s, v in b.w.items():
                if need.get(s, 0) < v:
                    need[s] = v
            for s, v in b.r.items():
                if need.get(s, 0) < v:
                    need[s] = v
        seen = self.seen[e]
        waits = []
        for s, v in need.items():
            if seen.get(s, 0) < v:
                seen[s] = v
                waits.append((s, v))
        return waits

    def _mark(self, ev, reads, writes):
        s, v = ev
        for b in writes:
            if isinstance(b, DBuf):
                b.w[s] = v
            else:
                b.w = {s: v}
                b.r = {}
        for b in reads:
            b.r[s] = v

    def op(self, e, fn, reads=(), writes=()):
        waits = self._deps(e, reads, writes)
        s = self.done[e]
        self.cnt[s] += 1
        ev = (s, self.cnt[s])
        self.stream[e].append((waits, fn, (s, 1)))
        self._mark(ev, reads, writes)

    def dma(self, q, sem, out_ap, in_ap, reads=(), writes=(), slow=False):
        waits = self._deps(q, reads, writes)
        self.cnt[sem] += 16
        ev = (sem, self.cnt[sem])
        kw = {"allow_slow_non_contiguous": True} if slow else {}
        nc = self.nc
        pref = {"sp": "SP", "pool": "Pool", "act": "Activation"}[q]
        etype = {"sp": mybir.EngineType.SP, "pool": mybir.EngineType.Pool, "act": mybir.EngineType.Activation}[q]

        def fn(eng):
            a = nc.next_id()
            ins = eng.dma_start(out=out_ap, in_=in_ap, **kw)
            if self.in_loop:
                b = nc.next_id()
                for cand in range(a, b + 1):
                    try:
                        eng.free_register(bass.RegisterHandle(f"{pref}_tmp_{cand}", etype))
                    except BaseException:
                        pass
            return ins
        self.stream[q].append((waits, fn, (sem, 16)))
        self._mark(ev, reads, writes)

    def barrier(self, engines=None, final=False):
        for e in (engines or self.eng):
            waits = []
            seen = self.seen[e]
            for s, v in self.cnt.items():
                if s in self.nobarrier and not final:
                    continue
                if v > 0 and seen.get(s, 0) < v:
                    seen[s] = v
                    waits.append((s, v))
            if waits:
                self.stream[e].append((waits, None, None))

    def flush(self, i=None, delta=None):
        for name, eng in self.eng.items():
            self.nflush = getattr(self, "nflush", 0) + 1
            if isinstance(i, int):
                delta = {k_: 0 for k_ in delta}
            tmp = eng.alloc_register(f"wtmp_{name}_{self.nflush}") if i is not None else None
            for waits, fn, inc in self.stream[name]:
                for s_, v in waits:
                    d = delta.get(s_, 0) if delta else 0
                    if d:
                        eng.reg_mul(tmp, i, d)
                        eng.reg_add(tmp, tmp, v)
                        eng.wait_ge(s_, tmp)
                    else:
                        eng.wait_ge(s_, v)
                if fn is not None:
                    ins = fn(eng)
                    if inc is not None:
                        ins.then_inc(inc[0], inc[1])
            if tmp is not None:
                eng.free_register(tmp)
            self.stream[name] = []


class Ring:
    ALL = []

    def __init__(self, S, name, aps):
        Ring.ALL.append(self)
        self.aps = aps
        self.bufs = [Buf(f"{name}{i}") for i in range(len(aps))]
        self.sems = [S.new_sem(f"{name}_s{i}") for i in range(len(aps))]
        self.i = 0

    def next(self):
        i = self.i % len(self.aps)
        self.i += 1
        return self.aps[i], self.bufs[i], self.sems[i]


class Prog:
    def __init__(self, T, nlayers=DEPTH, arena_kib=204, debug=False):
        self.debug = debug
        self.T = T
        self.NL = nlayers
        self.NT = T // 128
        self.NC = T // 512
        nc = bass.Bass("TRN2", target_bir_lowering=False)
        self.nc = nc
        self.stack = ExitStack()
        self.S = Sched(nc, self.stack)
        self.arena_elems = arena_kib * 1024 // 2
        self.arena = self.stack.enter_context(nc.sbuf_tensor("arena", [128, self.arena_elems], BF16))
        self.psum = self.stack.enter_context(nc.psum_tensor("psum", [128, 4096], F32))
        self.pbuf = [Buf(f"ps{i}") for i in range(8)]
        self.pbank_i = 0
        self.ap_ptr = 0
        self.sem_cache = {}
        self.declare()

    def alloc(self, n_elems, dtype=BF16):
        n2 = n_elems * (2 if dtype == F32 else 1)
        if dtype == F32 and self.ap_ptr % 2:
            self.ap_ptr += 1
        a = self.ap_ptr
        self.ap_ptr += n2
        assert self.ap_ptr <= self.arena_elems, f"SBUF arena overflow {self.ap_ptr*2/1024:.1f} KiB"
        ap = self.arena[:, a:a + n2]
        if dtype == F32:
            ap = ap.bitcast(F32)
        return ap

    def bank(self, i):
        return self.psum[:, i * 512:(i + 1) * 512]

    def sem(self, name):
        if name not in self.sem_cache:
            self.sem_cache[name] = self.S.new_sem(name, pooled=False)
        return self.sem_cache[name]

    def phase_end(self):
        self.S.barrier()
        self.S.flush()
        self.S.pool_i = 0
        self.ap_ptr = self.const_ptr

    def loop(self, n, body):
        S = self.S
        S.barrier()
        S.flush()
        Buf.reset_all()
        pre = dict(S.cnt)
        with self.nc.Fori(0, n) as i:
            self.nc.cur_bb.disable_value_cache()
            S.in_loop = True
            body(i)
            S.barrier()
            delta = {s_: S.cnt[s_] - pre.get(s_, 0) for s_ in S.cnt}
            S.flush(i, delta)
            S.in_loop = False
            loop_regs = [] if isinstance(i, int) else list(i.val.handles)
        for h_ in loop_regs:
            eng = self.nc.engines[h_.engine]
            for nm in (h_.name, h_.name.split("_snap_")[0]):
                try:
                    eng.free_register(bass.RegisterHandle(nm, h_.engine))
                except BaseException:
                    pass
        for s_ in S.cnt:
            S.cnt[s_] = pre.get(s_, 0) + n * delta[s_]
        for e in S.seen:
            for s_ in S.cnt:
                if s_ not in S.nobarrier:
                    S.seen[e][s_] = S.cnt[s_]
        Buf.reset_all()

    def dram(self, name, shape, dtype, kind="Internal"):
        if kind == "Internal" and self.debug and not name.startswith("wb_"):
            kind = "ExternalOutput"
        return self.nc.dram_tensor(name, list(shape), dtype, kind=kind).ap()

    def declare(self):
        T, NL, NC = self.T, self.NL, self.NC
        d = self.dram
        self.x_in = d("x", [NC, 512, D], F32, "ExternalInput")
        self.y_out = d("y", [NC, 512, D], F32, "ExternalOutput")
        self.w_in = d("w_in", [NL, D, INW], F32, "ExternalInput")
        self.w_ba = d("w_br_attn", [NL, D, D], F32, "ExternalInput")
        self.w_bf = d("w_br_fnet", [NL, FW, D], F32, "ExternalInput")
        self.w_out = d("w_out", [NL, D, D], F32, "ExternalInput")
        self.w_gu = d("w_gu", [NL, D, 2 * DFF], F32, "ExternalInput")
        self.w_dn = d("w_down", [NL, DFF, D], F32, "ExternalInput")
        self.rel_bias = d("rel_bias", [32, NH], F32, "ExternalInput")
        self.rel_bias_t = d("rel_bias_t", [NH, 32, 1], F32, "ExternalInput")
        self.ln_in_g = d("ln_in_g", [1, D], F32, "ExternalInput")
        self.ln_in_b = d("ln_in_b", [1, D], F32, "ExternalInput")
        self.b_gate = d("b_gate", [NL, 2 * D], F32, "ExternalInput")
        self.lam = d("lam", [NL, 4 * HD], F32, "ExternalInput")
        self.subln_g = d("subln_g", [NL, VD], F32, "ExternalInput")
        self.ln1_g = d("ln1_g", [NL, D], F32, "ExternalInput")
        self.ln1_b = d("ln1_b", [NL, D], F32, "ExternalInput")
        self.ln2_g = d("ln2_g", [NL, D], F32, "ExternalInput")
        self.ln2_b = d("ln2_b", [NL, D], F32, "ExternalInput")
        self.c_dft = d("c_dft", [NC, 128, self.NT, 2, 512], BF16, "ExternalInput")
        self.c_dftc = d("c_dftc", [128, 256], BF16, "ExternalInput")
        self.c_mask = d("c_mask", [128, self.NT * NC], F32, "ExternalInput")
        self.c_selm = d("c_selm", [128, self.NT * NC], F32, "ExternalInput")
        self.c_selp = d("c_selp", [128, self.NT * NC], F32, "ExternalInput")
        self.c_oh = d("c_oh", [32, TABW], F32, "ExternalInput")
        self.c_ident = d("c_ident", [128, 128], BF16, "ExternalInput")
        self.c_anti = d("c_anti", [128, 128], BF16, "ExternalInput")
        self.wb_in = d("wb_in", [NL, D, INW], BF16)
        self.wb_ba = d("wb_ba", [NL, D, D], BF16)
        self.wb_bf = d("wb_bf", [NL, FW, D], BF16)
        self.wb_out = d("wb_out", [NL, D, D], BF16)
        self.wb_gu = d("wb_gu", [NL, D, 2 * DFF], BF16)
        self.wb_dn = d("wb_dn", [NL, DFF, D], BF16)
        self.xres = d("xres", [NC, 512, D], F32)
        self.x1res = d("x1res", [NC, 512, D], F32)
        self.xT = d("xT", [16, 128, NC, 512], BF16)
        self.x1T = d("x1T", [16, 128, NC, 512], BF16)
        self.qkT = d("qkT", [2, NH, 2, 128, NC, 512], BF16)
        self.vA = d("vA", [NC, 512, NH, VD + 1], BF16)
        self.fT = d("fT", [NG, 128, NC, 512], BF16)
        self.oT = d("oT", [NH, 2, 128, NC, 512], BF16)
        self.yT = d("yT", [NG, 128, NC, 512], BF16)
        self.tab = d("tab", [NH, TABW], BF16)
        self.btx = d("btx", [NH, 6, 128, 512], BF16)
        self.db = {k: DBuf(k) for k in ("xres", "x1res", "xT", "x1T", "qkT", "vA", "fT", "oT", "yT", "tab", "btx", "y")}
        for l in range(NL):
            for k in ("wb_in", "wb_ba", "wb_bf", "wb_out", "wb_gu", "wb_dn"):
                self.db[f"{k}{l}"] = DBuf(f"{k}{l}", persist=True)

    def mm_group(self, mms, reads, writes):
        def fn(eng, mms=mms):
            ins = None
            for (o, l, r, st, sp) in mms:
                ins = eng.matmul(o, l, r, start=st, stop=sp)
            return ins
        self.S.op("pe", fn, reads, writes)

    def act(self, out, in_, func, reads, writes, bias=None, scale=None, accum_out=None):
        kw = {}
        if bias is not None:
            kw["bias"] = bias
        if scale is not None:
            kw["scale"] = scale
        if accum_out is not None:
            kw["accum_out"] = accum_out
        self.S.op("act", lambda e: e.activation(out, in_, func, **kw), reads, writes)

    def v(self, eng, meth, reads, writes, *a, **kw):
        self.S.op(eng, lambda e: getattr(e, meth)(*a, **kw), reads, writes)

    def build(self):
        S = self.S
        self.setup_consts()
        self.cast_weights()
        self.phase_ln_in()
        for l in range(self.NL):
            self.phase_p1(l)
            self.phase_attn(l)
            self.phase_fnet(l)
            self.phase_p4(l)
            self.phase_p5(l, last=(l == self.NL - 1))
        S.barrier(final=True)
        S.flush()
        return self.nc

    def setup_consts(self):
        S = self.S
        sem = self.sem("const")
        self.cb = Buf("consts", persist=True)
        self.ident = self.alloc(128)
        self.anti = self.alloc(128)
        self.dftc = self.alloc(256)
        self.relb_m = self.alloc(NH, F32)
        self.relb_p = self.alloc(NH, F32)
        self.eps_t = self.alloc(1, F32)
        cb = [self.cb]
        S.dma("sp", sem, self.ident, self.c_ident, writes=cb)
        S.dma("sp", sem, self.anti, self.c_anti, writes=cb)
        S.dma("sp", sem, self.dftc, self.c_dftc, writes=cb)
        S.dma("sp", sem, self.relb_m, self.rel_bias[15:16, :].partition_broadcast(128), writes=cb)
        S.dma("sp", sem, self.relb_p, self.rel_bias[31:32, :].partition_broadcast(128), writes=cb)
        self.neghalf = self.alloc(1, F32)
        self.v("dve", "memset", [], cb, self.eps_t, EPS)
        self.v("dve", "memset", [], cb, self.neghalf, -0.5)
        self.const_ptr = self.ap_ptr
        oh = self.alloc(TABW, F32)
        rb = self.alloc(NH, F32)
        tb = self.alloc(TABW)
        t = Buf("tabtmp")
        S.dma("sp", sem, oh[0:32, :], self.c_oh, writes=[t])
        S.dma("sp", sem, rb[0:32, :], self.rel_bias, writes=[t])
        for j in range(0, TABW, 512):
            w = min(512, TABW - j)
            pb = self.pbuf[0]
            self.mm_group([(self.bank(0)[0:NH, 0:w], rb[0:32, :], oh[0:32, j:j + w], True, True)], [t], [pb])
            t2 = Buf("tabtmp2")
            self.v("dve", "tensor_copy", [pb], [t2], tb[0:NH, j:j + w], self.bank(0)[0:NH, 0:w])
            S.dma("sp", sem, self.tab[:, j:j + w], tb[0:NH, j:j + w], reads=[t2], writes=[self.db["tab"]])
        for h in range(NH):
            for ri, r in enumerate(range(-1, 5)):
                base = 512 - 128 * r
                src = bass.AP(self.tab.tensor, self.tab.offset + h * TABW + base, [[1, 128], [1, 512]])
                S.dma("sp", sem, self.btx[h, ri], src, reads=[self.db["tab"]], writes=[self.db["btx"]])
        self.phase_end()

    def cast_weights(self):
        self.cast_list = []
        for l in range(self.NL):
            for name, src, dst, rows in (("wb_in", self.w_in, self.wb_in, D), ("wb_ba", self.w_ba, self.wb_ba, D),
                                         ("wb_bf", self.w_bf, self.wb_bf, FW), ("wb_out", self.w_out, self.wb_out, D),
                                         ("wb_gu", self.w_gu, self.wb_gu, D), ("wb_dn", self.w_dn, self.wb_dn, DFF)):
                step = 512
                sem = self.sem(f"wc_{name}{l}")
                self.S.nobarrier.add(sem)
                for r0 in range(0, rows, step):
                    r1 = min(rows, r0 + step)
                    s_ap = src[l, r0:r1, :].rearrange("(p a) n -> p (a n)", p=128)
                    d_ap = dst[l, r0:r1, :].rearrange("(p a) n -> p (a n)", p=128)
                    self.cast_list.append((sem, d_ap, s_ap, self.db[f"{name}{l}"]))
        self.cast_some(len(self.cast_list))

    def cast_some(self, n):
        for _ in range(min(n, len(self.cast_list))):
            sem, d_ap, s_ap, db = self.cast_list.pop(0)
            self.S.dma("pool", sem, d_ap, s_ap, writes=[db])

    def load_bcast(self, sem, vec_ap, buf):
        t = self.alloc(D, F32)
        self.S.dma("sp", sem, t, vec_ap.partition_broadcast(128), writes=[buf])
        return t

    def layernorm_tile(self, z, zb, G, B, gb_buf, xb, xb_buf, st_ap, mv_ap, tmp_buf):
        nchunk = D // 512
        for c in range(nchunk):
            self.v("dve", "bn_stats", [zb], [tmp_buf], st_ap[:, c * 6:(c + 1) * 6], z[:, c * 512:(c + 1) * 512])
        self.v("dve", "bn_aggr", [tmp_buf], [tmp_buf], mv_ap[:, 0:2],
               st_ap[:, 0:nchunk * 6].rearrange("p (c s) -> p c s", s=6))
        self.v("dve", "tensor_scalar", [tmp_buf], [tmp_buf], mv_ap[:, 4:5], mv_ap[:, 1:2], EPS, None, ALU.add)
        self.v("pool", "tensor_tensor", [tmp_buf, self.cb], [tmp_buf], mv_ap[:, 2:3], mv_ap[:, 4:5], self.neghalf, ALU.pow)
        self.v("dve", "scalar_tensor_tensor", [tmp_buf], [tmp_buf], mv_ap[:, 3:4], mv_ap[:, 0:1], -1.0, mv_ap[:, 2:3],
               ALU.mult, ALU.mult)
        self.act(z, z, AF.Identity, [zb, tmp_buf], [zb], bias=mv_ap[:, 3:4], scale=mv_ap[:, 2:3])
        self.v("dve", "tensor_tensor", [zb, gb_buf], [zb], z, z, G, ALU.mult)
        self.v("dve", "tensor_tensor", [zb, gb_buf], [zb], z, z, B, ALU.add)
        self.v("pool", "tensor_copy", [zb], [xb_buf], xb, z)

    def transpose_tile(self, xb, xb_buf, dstT, dst_buf, tok0):
        for a in range(4):
            bi = self.pbank_i % 8
            self.pbank_i += 1
            pb = self.pbuf[bi]
            pt = self.bank(bi).bitcast(BF16)[:, 0:512]

            def fn(eng, a=a, pt=pt):
                ins = None
                for j in range(4):
                    fc = 4 * a + j
                    ins = eng.transpose(pt[:, j * 128:(j + 1) * 128], xb[:, fc * 128:(fc + 1) * 128], self.ident)
                return ins
            self.S.op("pe", fn, [xb_buf, self.cb], [pb])
            self.v("dve", "tensor_copy", [pb], [dst_buf], dstT[:, 4 * a:4 * a + 4, tok0:tok0 + 128],
                   pt.rearrange("p (j t) -> p j t", j=4))

    def phase_ln_in(self):
        S = self.S
        sem = self.sem("gb")
        gb = Buf("gb")
        G = self.load_bcast(sem, self.ln_in_g, gb)
        B = self.load_bcast(sem, self.ln_in_b, gb)
        zring = Ring(S, "z0", [self.alloc(D, F32) for _ in range(3)])
        xbs = Ring(S, "xb0", [self.alloc(D) for _ in range(2)])
        xts = Ring(S, "xt0", [self.alloc(16 * 512).rearrange("p (c t) -> p c t", c=16) for _ in range(1)])
        st = self.alloc(32, F32)
        mv = self.alloc(8, F32)
        tmpb = Buf("lntmp")

        def body(tb):
            xt, xtb, xtsem = xts.next()
            for tt in range(4):
                z, zb, zsem = zring.next()
                S.dma("sp", zsem, z, self.x_in[tb, tt * 128:(tt + 1) * 128, :], writes=[zb])
                xb, xbb, _ = xbs.next()
                self.layernorm_tile(z, zb, G, B, gb, xb, xbb, st, mv, tmpb)
                S.dma("pool", zsem, self.xres[tb, tt * 128:(tt + 1) * 128, :], z, reads=[zb], writes=[self.db["xres"]])
                self.transpose_tile(xb, xbb, xt, xtb, tt * 128)
            S.dma("pool", xtsem, self.xT[:, :, tb].rearrange("c p t -> p c t"), xt, reads=[xtb], writes=[self.db["xT"]])
        UNR = 4 if self.NC % 4 == 0 else (2 if self.NC % 2 == 0 else 1)

        def body_u(i):
            for k in range(UNR):
                body(i * UNR + k if UNR > 1 else i)
        self.loop(self.NC // UNR, body_u)
        self.phase_end()

    def wslab(self, ring, wsrc, wdb, k0, nk, n0, ncols):
        ap, buf, sem = ring.next()
        view = ap[:, 0:nk * ncols].rearrange("p (k n) -> p k n", k=nk)
        self.S.dma("sp", sem, view, wsrc[k0 * 128:(k0 + nk) * 128, n0:n0 + ncols].rearrange("(k p) n -> p k n", p=128),
                   reads=[wdb], writes=[buf])
        return view, buf

    def next_bank(self):
        bi = self.pbank_i % 8
        self.pbank_i += 1
        return bi

    def phase_p1(self, l):
        S = self.S
        W = self.wb_in[l]
        wdb = self.db[f"wb_in{l}"]
        wring = Ring(S, "p1w", [self.alloc(16 * 512) for _ in range(3)])
        xring = Ring(S, "p1x", [self.alloc(16 * 512).rearrange("p (c t) -> p c t", c=16) for _ in range(2)])
        oring = Ring(S, "p1o", [self.alloc(4 * 512) for _ in range(3)])
        vring = Ring(S, "p1v", [self.alloc(4 * NH * (VD + 1)) for _ in range(2)])
        qscale = HD ** -0.5
        qk16 = self.qkT.rearrange("a h j p c t -> (a h j) p c t")
        UNR = 4 if self.NC % 4 == 0 else (2 if self.NC % 2 == 0 else 1)

        def block(tb):
            evi = 0
            xt, xtb, xsem = xring.next()
            S.dma("sp", xsem, xt, self.xT[:, :, tb].rearrange("c p t -> p c t"), reads=[self.db["xT"]], writes=[xtb])
            for (off, ncols, dst, dname, cbase, scale) in ((OFF_Q, 2048, qk16, "qkT", 0, qscale),
                                                            (OFF_K, 2048, qk16, "qkT", 16, None),
                                                            (OFF_F, 1024, self.fT, "fT", 0, None)):
                for s0 in range(0, ncols, 512):
                    wv, wbuf = self.wslab(wring, W, wdb, 0, 16, off + s0, 512)
                    ot, ob, osem = oring.next()
                    otv = ot.rearrange("p (f t) -> p f t", f=4)
                    for fo in range(4):
                        bi = self.next_bank()
                        pb = self.pbuf[bi]
                        mms = [(self.bank(bi), wv[:, kc, fo * 128:(fo + 1) * 128], xt[:, kc, :], kc == 0, kc == 15)
                               for kc in range(16)]
                        self.mm_group(mms, [wbuf, xtb], [pb])
                        if scale is not None:
                            self.act(otv[:, fo, :], self.bank(bi), AF.Copy, [pb], [ob], scale=scale)
                        elif evi % 2 == 0:
                            self.v("dve", "tensor_copy", [pb], [ob], otv[:, fo, :], self.bank(bi))
                        else:
                            self.act(otv[:, fo, :], self.bank(bi), AF.Copy, [pb], [ob])
                        evi += 1
                    c0 = cbase + s0 // 128
                    S.dma("pool", osem, dst[c0:c0 + 4, :, tb].rearrange("c p t -> p c t"), otv,
                          reads=[ob], writes=[self.db[dname]])
            vt, vb, vsem = vring.next()
            vtv = vt.rearrange("p (t h d) -> p t h d", t=4, h=NH)
            self.v("pool", "memset", [], [vb], vtv[:, :, :, VD:VD + 1], 1.0)
            for s0 in range(0, 2048, 512):
                wv, wbuf = self.wslab(wring, W, wdb, 0, 16, OFF_V + s0, 512)
                for tt in range(4):
                    bi = self.next_bank()
                    pb = self.pbuf[bi]
                    mms = [(self.bank(bi), xt[:, kc, tt * 128:(tt + 1) * 128], wv[:, kc, :], kc == 0, kc == 15)
                           for kc in range(16)]
                    self.mm_group(mms, [wbuf, xtb], [pb])
                    h0 = s0 // VD
                    self.v("dve", "tensor_copy", [pb], [vb], vtv[:, tt, h0:h0 + 2, 0:VD],
                           self.bank(bi).rearrange("p (h d) -> p h d", h=2))
            S.dma("pool", vsem, self.vA[tb].rearrange("(t p) h d -> p t h d", p=128), vtv,
                  reads=[vb], writes=[self.db["vA"]])

        def body(i):
            for k in range(UNR):
                block(i * UNR + k if UNR > 1 else i)
        self.loop(self.NC // UNR, body)
        self.phase_end()

    def phase_attn(self, l):
        S = self.S
        T, NT, NC = self.T, self.NT, self.NC
        lam_init = 0.8 - 0.6 * math.exp(-0.3 * l)
        sem = self.sem("attc")
        cb2 = Buf("attc", persist=True)
        lamt = self.alloc(4 * HD, F32)
        S.dma("sp", sem, lamt, self.lam[l:l + 1, :].partition_broadcast(128), writes=[cb2])
        sc = self.alloc(16, F32)
        junk = self.alloc(HD, F32)
        self.v("dve", "tensor_tensor", [cb2], [cb2], junk, lamt[:, 0:HD], lamt[:, HD:2 * HD], ALU.mult)
        self.v("dve", "reduce_sum", [cb2], [cb2], sc[:, 0:1], junk, AX.X)
        self.v("dve", "tensor_tensor", [cb2], [cb2], junk, lamt[:, 2 * HD:3 * HD], lamt[:, 3 * HD:4 * HD], ALU.mult)
        self.v("dve", "reduce_sum", [cb2], [cb2], sc[:, 1:2], junk, AX.X)
        self.act(sc[:, 2:4], sc[:, 0:2], AF.Exp, [cb2], [cb2])
        self.v("dve", "scalar_tensor_tensor", [cb2], [cb2], sc[:, 4:5], sc[:, 3:4], -lam_init, sc[:, 2:3],
               ALU.add, ALU.subtract)
        neglam = sc[:, 4:5]
        gcol = self.alloc(2, F32)
        S.dma("sp", sem, gcol, self.subln_g[l:l + 1, :].rearrange("o (c p) -> p (o c)", p=128), writes=[cb2], slow=True)
        self.v("dve", "tensor_scalar", [cb2], [cb2], gcol, gcol, 1.0 - lam_init, None, ALU.mult)
        ncol = NT * NC
        maskt = self.alloc(ncol, F32)
        selm = self.alloc(ncol, F32)
        selp = self.alloc(ncol, F32)
        S.dma("sp", sem, maskt, self.c_mask, writes=[cb2])
        S.dma("sp", sem, selm, self.c_selm, writes=[cb2])
        S.dma("sp", sem, selp, self.c_selp, writes=[cb2])
        btab = self.alloc(ncol, F32)
        btabb = Buf("btab")
        relb = Ring(S, "relb", [self.alloc(2, F32)])
        bt_ring = Ring(S, "btr", [self.alloc(6 * 512)])
        kring = Ring(S, "kT", [self.alloc(2 * T)])
        vring = Ring(S, "vA", [self.alloc(NT * (VD + 1))])
        qring = Ring(S, "qT", [self.alloc(2 * 512) for _ in range(3)])
        PT = self.alloc(NT * 512)
        ptb = [Buf(f"pt{k_}") for k_ in range(NT)]
        oring = Ring(S, "oacc", [self.alloc(2 * (VD + 1), F32) for _ in range(8)])
        ostage = Ring(S, "ostg", [self.alloc(2 * 512) for _ in range(2)])
        small = Ring(S, "osm", [self.alloc(16, F32) for _ in range(3)])
        otmp = Ring(S, "otmp", [self.alloc(VD, F32) for _ in range(3)])
        obf = Ring(S, "obf", [self.alloc(VD) for _ in range(3)])

        def body(h):
            rb, rbb, rbsem = relb.next()
            S.dma("sp", rbsem, rb[:, 0:1], self.rel_bias_t[h, 15:16, :].partition_broadcast(128), writes=[rbb])
            S.dma("sp", rbsem, rb[:, 1:2], self.rel_bias_t[h, 31:32, :].partition_broadcast(128), writes=[rbb])
            self.v("dve", "scalar_tensor_tensor", [cb2, rbb], [btabb], btab, selm, rb[:, 0:1], maskt, ALU.mult, ALU.add)
            self.v("dve", "scalar_tensor_tensor", [cb2, rbb, btabb], [btabb], btab, selp, rb[:, 1:2], btab,
                   ALU.mult, ALU.add)
            bt, btb, btsem = bt_ring.next()
            btv = bt.rearrange("p (r q) -> p r q", r=6)
            S.dma("sp", btsem, btv, self.btx[h].rearrange("r p q -> p r q"), reads=[self.db["btx"]], writes=[btb])
            kt_, kb, ksem = kring.next()
            ktv = kt_.rearrange("p (j t) -> p j t", j=2)
            S.dma("sp", ksem, ktv, self.qkT[1, h].rearrange("j p c t -> p j (c t)"), reads=[self.db["qkT"]], writes=[kb])
            va, vb, vsem = vring.next()
            vav = va.rearrange("p (t d) -> p t d", t=NT)
            for c8 in range(0, NC, 2):
                S.dma("sp", vsem, vav[:, c8 * 4:(c8 + 2) * 4, :],
                      self.vA[c8:c8 + 2, :, h, :].rearrange("c (t p) d -> p (c t) d", p=128),
                      reads=[self.db["vA"]], writes=[vb])
            pending = []
            oacc_by_c = {}
            prev = None
            steps = [(c, j) for c in range(NC) for j in range(2)]

            def tail_mms(kts):
                return [(self.bank(2 + qi)[:, 0:VD + 1], PT[:, kt * 512 + (2 + qi) * 128:kt * 512 + (3 + qi) * 128],
                         vav[:, kt, :], kt == 0, kt == NT - 1) for kt in kts for qi in range(2)]

            def evac(bank_ids, c, j):
                for bi_ in bank_ids:
                    oa, oab, _ = oacc_by_c[c][bi_]
                    oav = oa.rearrange("p (j d) -> p j d", j=2)
                    self.v("dve", "tensor_copy", [self.pbuf[bi_]], [oab], oav[:, j, :], self.bank(bi_)[:, 0:VD + 1])

            def make_epilogue(c):
                tiles = oacc_by_c[c]

                def epilogue():
                    og, ogb, ogsem = ostage.next()
                    ogv = og.rearrange("p (c q) -> p c q", c=2)
                    for qi in range(4):
                        oa, oab, _ = tiles[qi]
                        oav = oa.rearrange("p (j d) -> p j d", j=2)
                        sm, smb, _ = small.next()
                        ot, otb, _ = otmp.next()
                        ob_, obb, _ = obf.next()
                        self.v("dve", "reciprocal", [oab], [smb], sm[:, 0:2], oav[:, :, VD])
                        self.v("dve", "tensor_tensor", [smb, cb2], [smb], sm[:, 1:2], sm[:, 1:2], neglam, ALU.mult)
                        self.v("dve", "tensor_scalar", [oab, smb], [otb], ot, oav[:, 0, 0:VD], sm[:, 0:1], None, ALU.mult)
                        self.v("dve", "scalar_tensor_tensor", [oab, smb, otb], [otb], ot, oav[:, 1, 0:VD], sm[:, 1:2], ot,
                               ALU.mult, ALU.add)
                        self.v("dve", "tensor_tensor", [otb], [oab], oav[:, 0, 0:VD], ot, ot, ALU.mult)
                        self.v("dve", "reduce_sum", [oab], [smb], sm[:, 2:3], oav[:, 0, 0:VD], AX.X)
                        self.v("dve", "tensor_scalar", [smb], [smb], sm[:, 3:4], sm[:, 2:3], 1.0 / VD, EPS, ALU.mult, ALU.add)
                        self.v("pool", "tensor_tensor", [smb, self.cb], [smb], sm[:, 4:5], sm[:, 3:4], self.neghalf, ALU.pow)
                        self.v("dve", "tensor_scalar", [otb, smb], [obb], ob_, ot, sm[:, 4:5], None, ALU.mult)
                        bi = 4 + 2 * (self.pbank_i % 2)
                        self.pbank_i += 1
                        pb = self.pbuf[bi]
                        ptp = self.bank(bi).bitcast(BF16)[:, 0:256]

                        def fn(eng, ptp=ptp, ob_=ob_):
                            eng.transpose(ptp[:, 0:128], ob_[:, 0:128], self.ident)
                            return eng.transpose(ptp[:, 128:256], ob_[:, 128:256], self.ident)
                        S.op("pe", fn, [obb, self.cb], [pb])
                        for cc in range(2):
                            self.v("dve", "tensor_scalar", [pb, cb2], [ogb], ogv[:, cc, qi * 128:(qi + 1) * 128],
                                   ptp[:, cc * 128:(cc + 1) * 128], gcol[:, cc:cc + 1], None, ALU.mult)
                    S.dma("pool", ogsem, self.oT[h, :, :, c].rearrange("j p t -> p j t"), ogv,
                          reads=[ogb], writes=[self.db["oT"]])
                return epilogue

            qtv = None
            for (c, j) in steps:
                if j == 0:
                    qt, qb, qsem = qring.next()
                    qtv = qt.rearrange("p (j t) -> p j t", j=2)
                    S.dma("sp", qsem, qtv, self.qkT[0, h, :, :, c].rearrange("j p t -> p j t"),
                          reads=[self.db["qkT"]], writes=[qb])
                    oacc_by_c[c] = [oring.next() for _ in range(4)]
                far = lambda k_: not (-1 <= k_ - 4 * c <= 4)
                units = []
                kt = 0
                while kt < NT:
                    if kt % 2 == 0 and kt + 1 < NT and far(kt) and far(kt + 1):
                        units.append((kt, kt + 1))
                        kt += 2
                    else:
                        units.append((kt,))
                        kt += 1
                last = None
                for kts in units:
                    if prev is not None:
                        self.mm_group(tail_mms(kts), [ptb[k_] for k_ in kts] + [vb], [self.pbuf[2], self.pbuf[3]])
                    slot = self.pbank_i % 2
                    self.pbank_i += 1
                    b0 = 4 + 2 * slot
                    pbs = [self.pbuf[b0 + ui] for ui in range(len(kts))]
                    mms = []
                    rd = [kb, qb]
                    for ui, kt in enumerate(kts):
                        r = kt - 4 * c
                        near = -1 <= r <= 4
                        mms.append((self.bank(b0 + ui), ktv[:, j, kt * 128:(kt + 1) * 128], qtv[:, j, :], True, not near))
                        if near:
                            mms.append((self.bank(b0 + ui), self.anti, btv[:, r + 1, :], False, True))
                            rd += [btb, self.cb]
                    self.mm_group(mms, rd, pbs)
                    w = 512 * len(kts)
                    self.act(PT[:, kts[0] * 512:kts[0] * 512 + w], self.psum[:, b0 * 512:b0 * 512 + w], AF.Exp,
                             pbs + [btabb], [ptb[k_] for k_ in kts],
                             bias=btab[:, kts[0] * NC + c:kts[0] * NC + c + 1], scale=1.0)
                    if last is not None:
                        self.mm_group([(self.bank(qi)[:, 0:VD + 1], PT[:, kt * 512 + qi * 128:kt * 512 + (qi + 1) * 128],
                                        vav[:, kt, :], kt == 0, kt == NT - 1) for kt in last for qi in range(2)],
                                      [ptb[k_] for k_ in last] + [vb], [self.pbuf[0], self.pbuf[1]])
                    last = kts
                self.mm_group([(self.bank(qi)[:, 0:VD + 1], PT[:, kt * 512 + qi * 128:kt * 512 + (qi + 1) * 128],
                                vav[:, kt, :], kt == 0, kt == NT - 1) for kt in last for qi in range(2)],
                              [ptb[k_] for k_ in last] + [vb], [self.pbuf[0], self.pbuf[1]])
                evac([0, 1], c, j)
                if prev is not None:
                    evac([2, 3], prev[0], prev[1])
                if pending:
                    pending.pop(0)()
                if prev is not None and prev[1] == 1:
                    pending.append(make_epilogue(prev[0]))
                prev = (c, j)
            for kt in range(NT):
                self.mm_group(tail_mms((kt,)), [ptb[kt], vb], [self.pbuf[2], self.pbuf[3]])
            evac([2, 3], prev[0], prev[1])
            pending.append(make_epilogue(prev[0]))
            while pending:
                pending.pop(0)()
        self.loop(NH, body)
        self.phase_end()

    def phase_fnet(self, l):
        S = self.S
        T, NT, NC = self.T, self.NT, self.NC
        GH = 4
        NB = 8 if NT >= 8 else NT
        UNR = 4 if NC % 4 == 0 else (2 if NC % 2 == 0 else 1)
        xcs = self.alloc(NT * GH * 256).rearrange("p (t g d) -> p t g d", t=NT, g=GH)
        xcsb = Buf("xcs", persist=True)
        fring = Ring(S, "fTl", [self.alloc(T) for _ in range(2)])
        dring = Ring(S, "dft", [self.alloc(NB * 2 * 512) for _ in range(2)])
        ystage = Ring(S, "ystg", [self.alloc(512) for _ in range(3)])
        for gh in range(NG // GH):
            for g in range(GH):
                ft, fb, fsem = fring.next()
                S.dma("sp", fsem, ft, self.fT[gh * GH + g].rearrange("p c t -> p (c t)"), reads=[self.db["fT"]],
                      writes=[fb])
                for t2 in range(0, NT, 2):
                    bi = self.next_bank()
                    pb = self.pbuf[bi]
                    mms = [(self.bank(bi)[:, k * 256:(k + 1) * 256], ft[:, (t2 + k) * 128:(t2 + k + 1) * 128], self.dftc,
                            True, True) for k in range(2)]
                    self.mm_group(mms, [fb, self.cb], [pb])
                    if (t2 // 2) % 2 == 0:
                        self.v("dve", "tensor_copy", [pb], [xcsb], xcs[:, t2:t2 + 2, g, :],
                               self.bank(bi).rearrange("p (k d) -> p k d", k=2))
                    else:
                        self.v("act", "copy", [pb], [xcsb], xcs[:, t2:t2 + 2, g, :],
                               self.bank(bi).rearrange("p (k d) -> p k d", k=2))

            def chunk(kc, gh=gh):
                banks = [self.next_bank() for _ in range(GH)]
                for n0 in range(0, NT, NB):
                    dt_, dbf, dsem = dring.next()
                    dtv = dt_.rearrange("p (n s k) -> p n s k", n=NB, s=2)
                    S.dma("sp", dsem, dtv, self.c_dft[kc, :, n0:n0 + NB], writes=[dbf])
                    mms = []
                    for nn in range(NB):
                        nt = n0 + nn
                        for g in range(GH):
                            mms.append((self.bank(banks[g]), xcs[:, nt, g, 0:128], dtv[:, nn, 0, :], nt == 0, False))
                            mms.append((self.bank(banks[g]), xcs[:, nt, g, 128:256], dtv[:, nn, 1, :], False,
                                        nt == NT - 1))
                    self.mm_group(mms, [dbf, xcsb], [self.pbuf[b] for b in banks])
                for g in range(GH):
                    ys, ysb, yssem = ystage.next()
                    if g % 2 == 0:
                        self.v("dve", "tensor_copy", [self.pbuf[banks[g]]], [ysb], ys, self.bank(banks[g]))
                    else:
                        self.v("act", "copy", [self.pbuf[banks[g]]], [ysb], ys, self.bank(banks[g]))
                    S.dma("pool", yssem, self.yT[gh * GH + g, :, kc], ys, reads=[ysb], writes=[self.db["yT"]])

            def body(i):
                for k in range(UNR):
                    chunk(i * UNR + k if UNR > 1 else i)
            self.loop(NC // UNR, body)
        self.phase_end()

    def gemm_tm_ln(self, aT, aTb, nk, W, wdb, wring, zt, ztb, kslab=16, ncols=512):
        kslabs = [(k0, min(kslab, nk - k0)) for k0 in range(0, nk, kslab)]
        for s0 in range(0, D, ncols):
            banks = [self.next_bank() for _ in range(4)]
            for si, (k0, kn) in enumerate(kslabs):
                wv, wbuf = self.wslab(wring, W, wdb, k0, kn, s0, ncols)
                for tt in range(4):
                    mms = [(self.bank(banks[tt])[:, 0:ncols], aT[:, k0 + kk, tt * 128:(tt + 1) * 128], wv[:, kk, :],
                            (k0 + kk) == 0, (k0 + kk) == nk - 1) for kk in range(kn)]
                    self.mm_group(mms, [wbuf, aTb], [self.pbuf[banks[tt]]])
            for tt in range(4):
                self.v("dve", "scalar_tensor_tensor", [self.pbuf[banks[tt]], ztb], [ztb],
                       zt[:, tt, s0:s0 + ncols], zt[:, tt, s0:s0 + ncols], ALPHA, self.bank(banks[tt])[:, 0:ncols],
                       ALU.mult, ALU.add)

    def phase_p4(self, l):
        S = self.S
        W_in = self.wb_in[l]
        sem = self.sem("gb")
        gb = Buf("gb4", persist=True)
        G = self.load_bcast(sem, self.ln1_g[l:l + 1, :], gb)
        B = self.load_bcast(sem, self.ln1_b[l:l + 1, :], gb)
        bg = self.alloc(32, F32)
        S.dma("sp", sem, bg, self.b_gate[l:l + 1, :].rearrange("o (c p) -> p (o c)", p=128), writes=[gb], slow=True)
        aring = Ring(S, "p4a", [self.alloc(40 * 512) for _ in range(1)])
        wring = Ring(S, "p4w", [self.alloc(16 * 256) for _ in range(8)])
        mT = self.alloc(16 * 512).rearrange("p (c t) -> p c t", c=16)
        mTb = Buf("mT")
        zring = Ring(S, "p4z", [self.alloc(4 * D, F32) for _ in range(1)])
        sg = Ring(S, "p4sg", [self.alloc(512, F32) for _ in range(4)])
        xbs = Ring(S, "p4xb", [self.alloc(D) for _ in range(2)])
        xts = Ring(S, "p4xt", [self.alloc(16 * 512) for _ in range(1)])
        st = self.alloc(32, F32)
        mv = self.alloc(8, F32)
        tmpb = Buf("lntmp4")
        oT16 = self.oT.rearrange("h j p c t -> (h j) p c t")

        def body(tb):
            a, ab, asem = aring.next()
            av = a.rearrange("p (c t) -> p c t", c=40)
            S.dma("sp", asem, av[:, 0:16, :], self.xT[:, :, tb].rearrange("c p t -> p c t"),
                  reads=[self.db["xT"]], writes=[ab])
            S.dma("sp", asem, av[:, 16:32, :], oT16[:, :, tb].rearrange("c p t -> p c t"),
                  reads=[self.db["oT"]], writes=[ab])
            S.dma("sp", asem, av[:, 32:40, :], self.yT[:, :, tb].rearrange("c p t -> p c t"),
                  reads=[self.db["yT"]], writes=[ab])
            zt_, ztb, zsem = zring.next()
            zt = zt_.rearrange("p (t d) -> p t d", t=4)
            S.dma("sp", zsem, zt, self.xres[tb].rearrange("(t p) d -> p t d", p=128),
                  reads=[self.db["xres"]], writes=[ztb])
            for s0 in range(0, D, 256):
                wga, wgab = self.wslab(wring, W_in, self.db[f"wb_in{l}"], 0, 16, OFF_G + s0, 256)
                wya, wyab = self.wslab(wring, self.wb_ba[l], self.db[f"wb_ba{l}"], 0, 16, s0, 256)
                wgf, wgfb = self.wslab(wring, W_in, self.db[f"wb_in{l}"], 0, 16, OFF_G + D + s0, 256)
                wyf, wyfb = self.wslab(wring, self.wb_bf[l], self.db[f"wb_bf{l}"], 0, 8, s0, 256)
                for fo in range(2):
                    fc = s0 // 128 + fo
                    fs = slice(fo * 128, (fo + 1) * 128)
                    bks = [self.next_bank() for _ in range(4)]
                    pbs = [self.pbuf[b] for b in bks]
                    self.mm_group([(self.bank(bks[0]), wga[:, kc, fs], av[:, kc, :], kc == 0, kc == 15) for kc in range(16)],
                                  [wgab, ab], [pbs[0]])
                    self.mm_group([(self.bank(bks[1]), wya[:, kc, fs], av[:, 16 + kc, :], kc == 0, kc == 15) for kc in range(16)],
                                  [wyab, ab], [pbs[1]])
                    self.mm_group([(self.bank(bks[2]), wgf[:, kc, fs], av[:, kc, :], kc == 0, kc == 15) for kc in range(16)],
                                  [wgfb, ab], [pbs[2]])
                    self.mm_group([(self.bank(bks[3]), wyf[:, kc, fs], av[:, 32 + kc, :], kc == 0, kc == 7) for kc in range(8)],
                                  [wyfb, ab], [pbs[3]])
                    sa, sab, _ = sg.next()
                    sf, sfb, _ = sg.next()
                    self.act(sa, self.bank(bks[0]), AF.Sigmoid, [pbs[0], gb], [sab], bias=bg[:, fc:fc + 1], scale=1.0)
                    self.act(sf, self.bank(bks[2]), AF.Sigmoid, [pbs[2], gb], [sfb], bias=bg[:, 16 + fc:16 + fc + 1], scale=1.0)
                    self.v("dve", "tensor_tensor", [sab, pbs[1]], [sab], sa, sa, self.bank(bks[1]), ALU.mult)
                    self.v("dve", "tensor_tensor", [sfb, pbs[3]], [sfb], sf, sf, self.bank(bks[3]), ALU.mult)
                    self.v("pool", "tensor_tensor", [sab, sfb], [mTb], mT[:, fc, :], sa, sf, ALU.add)
            self.gemm_tm_ln(mT, mTb, 16, self.wb_out[l], self.db[f"wb_out{l}"], wring, zt, ztb, ncols=256)
            xt_, xtb, xtsem = xts.next()
            xt = xt_.rearrange("p (c t) -> p c t", c=16)
            for tt in range(4):
                xb, xbb, _ = xbs.next()
                self.layernorm_tile(zt[:, tt, :], ztb, G, B, gb, xb, xbb, st, mv, tmpb)
                self.transpose_tile(xb, xbb, xt, xtb, tt * 128)
            S.dma("pool", zsem, self.x1res[tb].rearrange("(t p) d -> p t d", p=128), zt, reads=[ztb],
                  writes=[self.db["x1res"]])
            S.dma("pool", xtsem, self.x1T[:, :, tb].rearrange("c p t -> p c t"), xt, reads=[xtb],
                  writes=[self.db["x1T"]])
        UNR = 4 if self.NC % 4 == 0 else (2 if self.NC % 2 == 0 else 1)

        def body_u(i):
            for k in range(UNR):
                body(i * UNR + k if UNR > 1 else i)
        self.loop(self.NC // UNR, body_u)
        self.phase_end()

    def phase_p5(self, l, last):
        S = self.S
        sem = self.sem("gb")
        gb = Buf("gb5", persist=True)
        G = self.load_bcast(sem, self.ln2_g[l:l + 1, :], gb)
        B = self.load_bcast(sem, self.ln2_b[l:l + 1, :], gb)
        NKF = DFF // 128
        aring = Ring(S, "p5a", [self.alloc(16 * 512) for _ in range(1)])
        wring = Ring(S, "p5w", [self.alloc(16 * 512) for _ in range(4)])
        hT = self.alloc(NKF * 512).rearrange("p (c t) -> p c t", c=NKF)
        hTb = Buf("hT")
        zring = Ring(S, "p5z", [self.alloc(4 * D, F32) for _ in range(1)])
        sg = Ring(S, "p5sg", [self.alloc(512, F32) for _ in range(3)])
        xbs = Ring(S, "p5xb", [self.alloc(D) for _ in range(2)])
        xts = Ring(S, "p5xt", [self.alloc(16 * 512) for _ in range(1)])
        st = self.alloc(32, F32)
        mv = self.alloc(8, F32)
        tmpb = Buf("lntmp5")
        Wgu = self.wb_gu[l]
        dst = self.y_out if last else self.xres
        dstdb = self.db["y"] if last else self.db["xres"]

        def body(tb):
            a, ab, asem = aring.next()
            av = a.rearrange("p (c t) -> p c t", c=16)
            S.dma("sp", asem, av, self.x1T[:, :, tb].rearrange("c p t -> p c t"), reads=[self.db["x1T"]], writes=[ab])
            zt_, ztb, zsem = zring.next()
            zt = zt_.rearrange("p (t d) -> p t d", t=4)
            S.dma("sp", zsem, zt, self.x1res[tb].rearrange("(t p) d -> p t d", p=128),
                  reads=[self.db["x1res"]], writes=[ztb])
            for s0 in range(0, DFF, 512):
                wg, wgb = self.wslab(wring, Wgu, self.db[f"wb_gu{l}"], 0, 16, s0, 512)
                wu, wub = self.wslab(wring, Wgu, self.db[f"wb_gu{l}"], 0, 16, DFF + s0, 512)
                for fo in range(4):
                    fc = s0 // 128 + fo
                    fs = slice(fo * 128, (fo + 1) * 128)
                    b0, b1 = self.next_bank(), self.next_bank()
                    self.mm_group([(self.bank(b0), wg[:, kc, fs], av[:, kc, :], kc == 0, kc == 15) for kc in range(16)],
                                  [wgb, ab], [self.pbuf[b0]])
                    self.mm_group([(self.bank(b1), wu[:, kc, fs], av[:, kc, :], kc == 0, kc == 15) for kc in range(16)],
                                  [wub, ab], [self.pbuf[b1]])
                    s_, sb_, _ = sg.next()
                    self.act(s_, self.bank(b0), AF.Silu, [self.pbuf[b0]], [sb_])
                    self.v("dve", "tensor_tensor", [sb_, self.pbuf[b1]], [hTb], hT[:, fc, :], s_, self.bank(b1), ALU.mult)
            self.gemm_tm_ln(hT, hTb, NKF, self.wb_dn[l], self.db[f"wb_dn{l}"], wring, zt, ztb, kslab=16)
            xt_, xtb, xtsem = xts.next()
            xt = xt_.rearrange("p (c t) -> p c t", c=16)
            for tt in range(4):
                xb, xbb, _ = xbs.next()
                self.layernorm_tile(zt[:, tt, :], ztb, G, B, gb, xb, xbb, st, mv, tmpb)
                if not last:
                    self.transpose_tile(xb, xbb, xt, xtb, tt * 128)
            S.dma("pool", zsem, dst[tb].rearrange("(t p) d -> p t d", p=128), zt, reads=[ztb], writes=[dstdb])
            if not last:
                S.dma("pool", xtsem, self.xT[:, :, tb].rearrange("c p t -> p c t"), xt, reads=[xtb],
                      writes=[self.db["xT"]])
        UNR = 4 if self.NC % 4 == 0 else (2 if self.NC % 2 == 0 else 1)

        def body_u(i):
            for k in range(UNR):
                body(i * UNR + k if UNR > 1 else i)
        self.loop(self.NC // UNR, body_u)
        self.phase_end()


def _rel_bucket_np(rel):
    nb = 16
    ret = np.where(rel > 0, nb, 0)
    n = np.abs(rel)
    max_exact = nb // 2
    nf = np.maximum(n, 1).astype(np.float32)
    large = max_exact + (np.log(nf / np.float32(max_exact)) / np.float32(math.log(128 / max_exact))
                         * np.float32(nb - max_exact)).astype(np.int32)
    large = np.minimum(large, nb - 1)
    return ret + np.where(n < max_exact, n, large)


def host_consts(T, seq_len):
    NT, NC = T // 128, T // 512
    bf = ml_dtypes.bfloat16
    n = np.arange(T)
    seg = n // seq_len
    dft = np.zeros((NC, 128, NT, 2, 512), dtype=bf)
    pos = (n % seq_len).astype(np.int64)
    for kc in range(NC):
        k = np.arange(kc * 512, (kc + 1) * 512)
        same = (seg[:, None] == seg[None, k])
        ang = 2.0 * np.pi * ((pos[:, None] * pos[None, k]) % seq_len) / seq_len
        cm = np.where(same, np.cos(ang), 0.0) / math.sqrt(seq_len)
        sm = np.where(same, np.sin(ang), 0.0) / math.sqrt(seq_len)
        dft[kc, :, :, 0, :] = cm.reshape(NT, 128, 512).transpose(1, 0, 2).astype(bf)
        dft[kc, :, :, 1, :] = sm.reshape(NT, 128, 512).transpose(1, 0, 2).astype(bf)
    cc = np.arange(128)
    angc = 2.0 * np.pi * ((cc[:, None] * cc[None, :]) % 128) / 128
    dftc = np.concatenate([np.cos(angc), -np.sin(angc)], axis=1) / math.sqrt(128)
    kt = np.arange(NT)[:, None]
    c = np.arange(NC)[None, :]
    same_seq = (kt * 128) // seq_len == (c * 512) // seq_len
    mask = np.where(same_seq, 0.0, MASKV)
    selm = (kt < 4 * c - 1).astype(np.float32)
    selp = (kt > 4 * c + 4).astype(np.float32)
    rep = lambda a: np.ascontiguousarray(np.broadcast_to(a.reshape(1, -1).astype(np.float32), (128, NT * NC)))
    u = np.arange(TABW)
    bucket = _rel_bucket_np((639 - u).astype(np.int32))
    oh = np.zeros((32, TABW), np.float32)
    oh[bucket, u] = 1.0
    ident = np.eye(128, dtype=np.float32)
    return {
        "c_dft": dft, "c_dftc": dftc.astype(bf), "c_mask": rep(mask), "c_selm": rep(selm), "c_selp": rep(selp),
        "c_oh": oh, "c_ident": ident.astype(bf), "c_anti": ident[::-1].copy().astype(bf),
    }


_PROG_CACHE = {}


def get_prog(T, nlayers=DEPTH, debug=False):
    key = (T, nlayers, debug)
    if key not in _PROG_CACHE:
        p = Prog(T, nlayers, debug=debug)
        p.build()
        _PROG_CACHE[key] = p
    return _PROG_CACHE[key]


def make_in_map(x_core, seq_len, params, consts_cache):
    T = x_core.shape[0]
    ck = (T, seq_len)
    if ck not in consts_cache:
        consts_cache[ck] = host_consts(T, seq_len)
    m = {"x": np.ascontiguousarray(x_core, dtype=np.float32).reshape(T // 512, 512, D)}
    m.update(params)
    m.update(consts_cache[ck])
    return m


def prep_params(rel_bias, ln_in_g, ln_in_b, w_in, b_gate, lam, subln_g, w_br_attn, w_br_fnet, w_out,
                ln1_g, ln1_b, w_gu, w_down, ln2_g, ln2_b):
    f = lambda a: np.ascontiguousarray(np.asarray(a, dtype=np.float32))
    nl = np.asarray(w_in).shape[0]
    return {
        "rel_bias": f(rel_bias), "rel_bias_t": f(np.asarray(rel_bias).T).reshape(NH, 32, 1), "ln_in_g": f(ln_in_g).reshape(1, D), "ln_in_b": f(ln_in_b).reshape(1, D),
        "w_in": f(w_in), "b_gate": f(b_gate), "lam": f(lam).reshape(nl, 4 * HD), "subln_g": f(subln_g),
        "w_br_attn": f(w_br_attn), "w_br_fnet": f(w_br_fnet), "w_out": f(w_out),
        "ln1_g": f(ln1_g), "ln1_b": f(ln1_b), "w_gu": f(w_gu), "w_down": f(w_down),
        "ln2_g": f(ln2_g), "ln2_b": f(ln2_b),
    }


def kernel(x_prompt, x_sample, **params):
    x_prompt = np.asarray(x_prompt)
    x_sample = np.asarray(x_sample)
    P = prep_params(**params)
    T = 8192
    prog = get_prog(T)
    cache = {}
    in_maps = []
    for i in range(4):
        in_maps.append(make_in_map(x_sample[i], 8192, P, cache))
    for i in range(4):
        in_maps.append(make_in_map(x_prompt[4 * i:4 * i + 4].reshape(T, D), 2048, P, cache))
    res = run_bass_kernel_spmd(prog.nc, in_maps, core_ids=list(range(8)))
    ys = [np.asarray(r["y"], dtype=np.float32).reshape(T, D) for r in res.results]
    y_sample = np.stack(ys[0:4], axis=0)
    y_prompt = np.concatenate([y.reshape(4, 2048, D) for y in ys[4:8]], axis=0)
    return (y_prompt, y_sample)
```

```python
import math
from contextlib import ExitStack

import numpy as np
import ml_dtypes

import concourse.bass as bass
import concourse.mybir as mybir
from concourse.bass_utils import run_bass_kernel_spmd

F32 = mybir.dt.float32
BF16 = mybir.dt.bfloat16
AF = mybir.ActivationFunctionType
ALU = mybir.AluOpType
AX = mybir.AxisListType

D = 2048
NH = 8
HD = 128
VD = 256
NG = 8
GD = 128
FW = 1024
INW = 11264
DFF = 5632
DEPTH = 2
ALPHA = (2.0 * DEPTH) ** 0.25
EPS = 1e-5
MASKV = -30000.0
OFF_Q, OFF_K, OFF_V, OFF_F, OFF_G = 0, 2048, 4096, 6144, 7168
TABW = 1280


class Buf:
    __slots__ = ("name", "w", "r", "persist")
    ALL = []

    def __init__(self, name, persist=False):
        self.name = name
        self.w = {}
        self.r = {}
        self.persist = persist
        Buf.ALL.append(self)

    @staticmethod
    def reset_all():
        for b in Buf.ALL:
            if not b.persist:
                b.w = {}
                b.r = {}


class DBuf(Buf):
    __slots__ = ()


class Sched:
    CE = ("pe", "act", "dve", "pool")

    def __init__(self, nc, stack):
        self.nc = nc
        self.stack = stack
        self.eng = dict(pe=nc.tensor, act=nc.scalar, dve=nc.vector, pool=nc.gpsimd, sp=nc.sync)
        self.stream = {k: [] for k in self.eng}
        self.done = {k: stack.enter_context(nc.semaphore("done_" + k)) for k in self.CE}
        self.cnt = {self.done[k]: 0 for k in self.CE}
        self.seen = {k: {} for k in self.eng}
        self.nsem = 0
        self.pool = []
        self.pool_i = 0
        self.nobarrier = set()
        self.in_loop = False

    def new_sem(self, name, pooled=True):
        if pooled:
            if self.pool_i < len(self.pool):
                s = self.pool[self.pool_i]
                self.pool_i += 1
                return s
            name = f"pool{len(self.pool)}"
        s = self.stack.enter_context(self.nc.semaphore(name))
        self.cnt[s] = 0
        self.nsem += 1
        if pooled:
            self.pool.append(s)
            self.pool_i += 1
        return s

    def _deps(self, e, reads, writes):
        need = {}
        for b in reads:
            for s, v in b.w.items():
                if need.get(s, 0) < v:
                    need[s] = v
        for b in writes:
            for s, v in b.w.items():
                if need.get(s, 0) < v:
                    need[s] = v
            for s, v in b.r.items():
                if need.get(s, 0) < v:
                    need[s] = v
        seen = self.seen[e]
        waits = []
        for s, v in need.items():
            if seen.get(s, 0) < v:
                seen[s] = v
                waits.append((s, v))
        return waits

    def _mark(self, ev, reads, writes):
        s, v = ev
        for b in writes:
            if isinstance(b, DBuf):
                b.w[s] = v
            else:
                b.w = {s: v}
                b.r = {}
        for b in reads:
            b.r[s] = v

    def op(self, e, fn, reads=(), writes=()):
        waits = self._deps(e, reads, writes)
        s = self.done[e]
        self.cnt[s] += 1
        ev = (s, self.cnt[s])
        self.stream[e].append((waits, fn, (s, 1)))
        self._mark(ev, reads, writes)

    def dma(self, q, sem, out_ap, in_ap, reads=(), writes=(), slow=False):
        waits = self._deps(q, reads, writes)
        self.cnt[sem] += 16
        ev = (sem, self.cnt[sem])
        kw = {"allow_slow_non_contiguous": True} if slow else {}
        nc = self.nc
        pref = {"sp": "SP", "pool": "Pool", "act": "Activation"}[q]
        etype = {"sp": mybir.EngineType.SP, "pool": mybir.EngineType.Pool, "act": mybir.EngineType.Activation}[q]

        def fn(eng):
            a = nc.next_id()
            ins = eng.dma_start(out=out_ap, in_=in_ap, **kw)
            if self.in_loop:
                b = nc.next_id()
                for cand in range(a, b + 1):
                    try:
                        eng.free_register(bass.RegisterHandle(f"{pref}_tmp_{cand}", etype))
                    except BaseException:
                        pass
            return ins
        self.stream[q].append((waits, fn, (sem, 16)))
        self._mark(ev, reads, writes)

    def barrier(self, engines=None, final=False):
        for e in (engines or self.eng):
            waits = []
            seen = self.seen[e]
            for s, v in self.cnt.items():
                if s in self.nobarrier and not final:
                    continue
                if v > 0 and seen.get(s, 0) < v:
                    seen[s] = v
                    waits.append((s, v))
            if waits:
                self.stream[e].append((waits, None, None))

    def flush(self, i=None, delta=None):
        for name, eng in self.eng.items():
            self.nflush = getattr(self, "nflush", 0) + 1
            if isinstance(i, int):
                delta = {k_: 0 for k_ in delta}
            tmp = None
            base = {}
            if i is not None and not isinstance(i, int):
                tmp = eng.alloc_register(f"wtmp_{name}_{self.nflush}")
                used = []
                for waits, fn, inc in self.stream[name]:
                    for s_, v in waits:
                        if delta.get(s_, 0) and s_ not in used:
                            used.append(s_)
                for k_, s_ in enumerate(used[:24]):
                    try:
                        r_ = eng.alloc_register(f"wbase_{name}_{self.nflush}_{k_}")
                    except BaseException:
                        break
                    eng.reg_mul(r_, i, delta[s_])
                    base[s_] = r_
            for waits, fn, inc in self.stream[name]:
                for s_, v in waits:
                    d = delta.get(s_, 0) if delta else 0
                    if d and s_ in base:
                        eng.reg_add(tmp, base[s_], v)
                        eng.wait_ge(s_, tmp)
                    elif d:
                        eng.reg_mul(tmp, i, d)
                        eng.reg_add(tmp, tmp, v)
                        eng.wait_ge(s_, tmp)
                    else:
                        eng.wait_ge(s_, v)
                if fn is not None:
                    ins = fn(eng)
                    if inc is not None:
                        ins.then_inc(inc[0], inc[1])
            for r_ in base.values():
                eng.free_register(r_)
            if tmp is not None:
                eng.free_register(tmp)
            self.stream[name] = []


class Ring:
    ALL = []

    def __init__(self, S, name, aps):
        Ring.ALL.append(self)
        self.aps = aps
        self.bufs = [Buf(f"{name}{i}") for i in range(len(aps))]
        self.sems = [S.new_sem(f"{name}_s{i}") for i in range(len(aps))]
        self.i = 0

    def next(self):
        i = self.i % len(self.aps)
        self.i += 1
        return self.aps[i], self.bufs[i], self.sems[i]


class Prog:
    def __init__(self, T, nlayers=DEPTH, arena_kib=204, debug=False):
        self.debug = debug
        self.T = T
        self.NL = nlayers
        self.NT = T // 128
        self.NC = T // 512
        nc = bass.Bass("TRN2", target_bir_lowering=False)
        self.nc = nc
        self.stack = ExitStack()
        self.S = Sched(nc, self.stack)
        self.arena_elems = arena_kib * 1024 // 2
        self.arena = self.stack.enter_context(nc.sbuf_tensor("arena", [128, self.arena_elems], BF16))
        self.psum = self.stack.enter_context(nc.psum_tensor("psum", [128, 4096], F32))
        self.pbuf = [Buf(f"ps{i}") for i in range(8)]
        self.pbank_i = 0
        self.ap_ptr = 0
        self.sem_cache = {}
        self.declare()

    def alloc(self, n_elems, dtype=BF16):
        n2 = n_elems * (2 if dtype == F32 else 1)
        if dtype == F32 and self.ap_ptr % 2:
            self.ap_ptr += 1
        a = self.ap_ptr
        self.ap_ptr += n2
        assert self.ap_ptr <= self.arena_elems, f"SBUF arena overflow {self.ap_ptr*2/1024:.1f} KiB"
        ap = self.arena[:, a:a + n2]
        if dtype == F32:
            ap = ap.bitcast(F32)
        return ap

    def bank(self, i):
        return self.psum[:, i * 512:(i + 1) * 512]

    def sem(self, name):
        if name not in self.sem_cache:
            self.sem_cache[name] = self.S.new_sem(name, pooled=False)
        return self.sem_cache[name]

    def phase_end(self):
        self.S.barrier()
        self.S.flush()
        self.S.pool_i = 0
        self.ap_ptr = self.const_ptr

    def loop(self, n, body):
        S = self.S
        S.barrier()
        S.flush()
        Buf.reset_all()
        pre = dict(S.cnt)
        with self.nc.Fori(0, n) as i:
            self.nc.cur_bb.disable_value_cache()
            S.in_loop = True
            body(i)
            S.barrier()
            delta = {s_: S.cnt[s_] - pre.get(s_, 0) for s_ in S.cnt}
            S.flush(i, delta)
            S.in_loop = False
            loop_regs = [] if isinstance(i, int) else list(i.val.handles)
        for h_ in loop_regs:
            eng = self.nc.engines[h_.engine]
            for nm in (h_.name, h_.name.split("_snap_")[0]):
                try:
                    eng.free_register(bass.RegisterHandle(nm, h_.engine))
                except BaseException:
                    pass
        for s_ in S.cnt:
            S.cnt[s_] = pre.get(s_, 0) + n * delta[s_]
        for e in S.seen:
            for s_ in S.cnt:
                if s_ not in S.nobarrier:
                    S.seen[e][s_] = S.cnt[s_]
        Buf.reset_all()

    def dram(self, name, shape, dtype, kind="Internal"):
        if kind == "Internal" and self.debug and not name.startswith("wb_"):
            kind = "ExternalOutput"
        return self.nc.dram_tensor(name, list(shape), dtype, kind=kind).ap()

    def declare(self):
        T, NL, NC = self.T, self.NL, self.NC
        d = self.dram
        self.x_in = d("x", [NC, 512, D], F32, "ExternalInput")
        self.y_out = d("y", [NC, 512, D], F32, "ExternalOutput")
        self.w_in = d("w_in", [NL, D, INW], F32, "ExternalInput")
        self.w_ba = d("w_br_attn", [NL, D, D], F32, "ExternalInput")
        self.w_bf = d("w_br_fnet", [NL, FW, D], F32, "ExternalInput")
        self.w_out = d("w_out", [NL, D, D], F32, "ExternalInput")
        self.w_gu = d("w_gu", [NL, D, 2 * DFF], F32, "ExternalInput")
        self.w_dn = d("w_down", [NL, DFF, D], F32, "ExternalInput")
        self.rel_bias = d("rel_bias", [32, NH], F32, "ExternalInput")
        self.rel_bias_t = d("rel_bias_t", [NH, 32, 1], F32, "ExternalInput")
        self.ln_in_g = d("ln_in_g", [1, D], F32, "ExternalInput")
        self.ln_in_b = d("ln_in_b", [1, D], F32, "ExternalInput")
        self.b_gate = d("b_gate", [NL, 2 * D], F32, "ExternalInput")
        self.lam = d("lam", [NL, 4 * HD], F32, "ExternalInput")
        self.subln_g = d("subln_g", [NL, VD], F32, "ExternalInput")
        self.ln1_g = d("ln1_g", [NL, D], F32, "ExternalInput")
        self.ln1_b = d("ln1_b", [NL, D], F32, "ExternalInput")
        self.ln2_g = d("ln2_g", [NL, D], F32, "ExternalInput")
        self.ln2_b = d("ln2_b", [NL, D], F32, "ExternalInput")
        self.c_dft = d("c_dft", [NC, 128, self.NT, 2, 512], BF16, "ExternalInput")
        self.c_dftc = d("c_dftc", [128, 256], BF16, "ExternalInput")
        self.c_mask = d("c_mask", [128, self.NT * NC], F32, "ExternalInput")
        self.c_selm = d("c_selm", [128, self.NT * NC], F32, "ExternalInput")
        self.c_selp = d("c_selp", [128, self.NT * NC], F32, "ExternalInput")
        self.c_oh = d("c_oh", [32, TABW], F32, "ExternalInput")
        self.c_ident = d("c_ident", [128, 128], BF16, "ExternalInput")
        self.c_anti = d("c_anti", [128, 128], BF16, "ExternalInput")
        self.wb_in = d("wb_in", [NL, D, INW], BF16)
        self.wb_ba = d("wb_ba", [NL, D, D], BF16)
        self.wb_bf = d("wb_bf", [NL, FW, D], BF16)
        self.wb_out = d("wb_out", [NL, D, D], BF16)
        self.wb_gu = d("wb_gu", [NL, D, 2 * DFF], BF16)
        self.wb_dn = d("wb_dn", [NL, DFF, D], BF16)
        self.xres = d("xres", [NC, 512, D], F32)
        self.x1res = d("x1res", [NC, 512, D], F32)
        self.xT = d("xT", [16, 128, NC, 512], BF16)
        self.x1T = d("x1T", [16, 128, NC, 512], BF16)
        self.qkT = d("qkT", [2, NH, 2, 128, NC, 512], BF16)
        self.vA = d("vA", [NC, 512, NH, VD + 1], BF16)
        self.fT = d("fT", [NG, 128, NC, 512], BF16)
        self.oT = d("oT", [NH, 2, 128, NC, 512], BF16)
        self.yT = d("yT", [NG, 128, NC, 512], BF16)
        self.tab = d("tab", [NH, TABW], BF16)
        self.btx = d("btx", [NH, 6, 128, 512], BF16)
        self.db = {k: DBuf(k) for k in ("xres", "x1res", "xT", "x1T", "qkT", "vA", "fT", "oT", "yT", "tab", "btx", "y")}
        for l in range(NL):
            for k in ("wb_in", "wb_ba", "wb_bf", "wb_out", "wb_gu", "wb_dn"):
                self.db[f"{k}{l}"] = DBuf(f"{k}{l}", persist=True)

    def mm_group(self, mms, reads, writes):
        def fn(eng, mms=mms):
            ins = None
            for (o, l, r, st, sp) in mms:
                ins = eng.matmul(o, l, r, start=st, stop=sp)
            return ins
        self.S.op("pe", fn, reads, writes)

    def act(self, out, in_, func, reads, writes, bias=None, scale=None, accum_out=None):
        kw = {}
        if bias is not None:
            kw["bias"] = bias
        if scale is not None:
            kw["scale"] = scale
        if accum_out is not None:
            kw["accum_out"] = accum_out
        self.S.op("act", lambda e: e.activation(out, in_, func, **kw), reads, writes)

    def v(self, eng, meth, reads, writes, *a, **kw):
        self.S.op(eng, lambda e: getattr(e, meth)(*a, **kw), reads, writes)

    def build(self):
        S = self.S
        self.setup_consts()
        self.cast_weights()
        self.phase_ln_in()
        for l in range(self.NL):
            self.phase_p1(l)
            self.phase_attn(l)
            self.phase_fnet(l)
            self.phase_p4(l)
            self.phase_p5(l, last=(l == self.NL - 1))
        S.barrier(final=True)
        S.flush()
        return self.nc

    def setup_consts(self):
        S = self.S
        sem = self.sem("const")
        self.cb = Buf("consts", persist=True)
        self.ident = self.alloc(128)
        self.anti = self.alloc(128)
        self.dftc = self.alloc(256)
        self.relb_m = self.alloc(NH, F32)
        self.relb_p = self.alloc(NH, F32)
        self.eps_t = self.alloc(1, F32)
        cb = [self.cb]
        S.dma("sp", sem, self.ident, self.c_ident, writes=cb)
        S.dma("sp", sem, self.anti, self.c_anti, writes=cb)
        S.dma("sp", sem, self.dftc, self.c_dftc, writes=cb)
        S.dma("sp", sem, self.relb_m, self.rel_bias[15:16, :].partition_broadcast(128), writes=cb)
        S.dma("sp", sem, self.relb_p, self.rel_bias[31:32, :].partition_broadcast(128), writes=cb)
        self.neghalf = self.alloc(1, F32)
        self.v("dve", "memset", [], cb, self.eps_t, EPS)
        self.v("dve", "memset", [], cb, self.neghalf, -0.5)
        self.const_ptr = self.ap_ptr
        oh = self.alloc(TABW, F32)
        rb = self.alloc(NH, F32)
        tb = self.alloc(TABW)
        t = Buf("tabtmp")
        S.dma("sp", sem, oh[0:32, :], self.c_oh, writes=[t])
        S.dma("sp", sem, rb[0:32, :], self.rel_bias, writes=[t])
        for j in range(0, TABW, 512):
            w = min(512, TABW - j)
            pb = self.pbuf[0]
            self.mm_group([(self.bank(0)[0:NH, 0:w], rb[0:32, :], oh[0:32, j:j + w], True, True)], [t], [pb])
            t2 = Buf("tabtmp2")
            self.v("dve", "tensor_copy", [pb], [t2], tb[0:NH, j:j + w], self.bank(0)[0:NH, 0:w])
            S.dma("sp", sem, self.tab[:, j:j + w], tb[0:NH, j:j + w], reads=[t2], writes=[self.db["tab"]])
        for h in range(NH):
            for ri, r in enumerate(range(-1, 5)):
                base = 512 - 128 * r
                src = bass.AP(self.tab.tensor, self.tab.offset + h * TABW + base, [[1, 128], [1, 512]])
                S.dma("sp", sem, self.btx[h, ri], src, reads=[self.db["tab"]], writes=[self.db["btx"]])
        self.phase_end()

    def cast_weights(self):
        self.cast_list = []
        for l in range(self.NL):
            for name, src, dst, rows in (("wb_in", self.w_in, self.wb_in, D), ("wb_ba", self.w_ba, self.wb_ba, D),
                                         ("wb_bf", self.w_bf, self.wb_bf, FW), ("wb_out", self.w_out, self.wb_out, D),
                                         ("wb_gu", self.w_gu, self.wb_gu, D), ("wb_dn", self.w_dn, self.wb_dn, DFF)):
                step = 512
                sem = self.sem(f"wc_{name}{l}")
                self.S.nobarrier.add(sem)
                for r0 in range(0, rows, step):
                    r1 = min(rows, r0 + step)
                    s_ap = src[l, r0:r1, :].rearrange("(p a) n -> p (a n)", p=128)
                    d_ap = dst[l, r0:r1, :].rearrange("(p a) n -> p (a n)", p=128)
                    self.cast_list.append((sem, d_ap, s_ap, self.db[f"{name}{l}"]))
        self.cast_some(len(self.cast_list))

    def cast_some(self, n):
        for _ in range(min(n, len(self.cast_list))):
            sem, d_ap, s_ap, db = self.cast_list.pop(0)
            self.S.dma("pool", sem, d_ap, s_ap, writes=[db])

    def load_bcast(self, sem, vec_ap, buf):
        t = self.alloc(D, F32)
        self.S.dma("sp", sem, t, vec_ap.partition_broadcast(128), writes=[buf])
        return t

    def layernorm_tile(self, z, zb, G, B, gb_buf, xb, xb_buf, st_ap, mv_ap, tmp_buf):
        nchunk = D // 512
        for c in range(nchunk):
            self.v("dve", "bn_stats", [zb], [tmp_buf], st_ap[:, c * 6:(c + 1) * 6], z[:, c * 512:(c + 1) * 512])
        self.v("dve", "bn_aggr", [tmp_buf], [tmp_buf], mv_ap[:, 0:2],
               st_ap[:, 0:nchunk * 6].rearrange("p (c s) -> p c s", s=6))
        self.v("dve", "tensor_scalar", [tmp_buf], [tmp_buf], mv_ap[:, 4:5], mv_ap[:, 1:2], EPS, None, ALU.add)
        self.v("pool", "tensor_tensor", [tmp_buf, self.cb], [tmp_buf], mv_ap[:, 2:3], mv_ap[:, 4:5], self.neghalf, ALU.pow)
        self.v("dve", "scalar_tensor_tensor", [tmp_buf], [tmp_buf], mv_ap[:, 3:4], mv_ap[:, 0:1], -1.0, mv_ap[:, 2:3],
               ALU.mult, ALU.mult)
        self.act(z, z, AF.Identity, [zb, tmp_buf], [zb], bias=mv_ap[:, 3:4], scale=mv_ap[:, 2:3])
        self.v("dve", "tensor_tensor", [zb, gb_buf], [zb], z, z, G, ALU.mult)
        self.v("dve", "tensor_tensor", [zb, gb_buf], [zb], z, z, B, ALU.add)
        self.v("pool", "tensor_copy", [zb], [xb_buf], xb, z)

    def transpose_tile(self, xb, xb_buf, dstT, dst_buf, tok0):
        for a in range(4):
            bi = self.pbank_i % 8
            self.pbank_i += 1
            pb = self.pbuf[bi]
            pt = self.bank(bi).bitcast(BF16)[:, 0:512]

            def fn(eng, a=a, pt=pt):
                ins = None
                for j in range(4):
                    fc = 4 * a + j
                    ins = eng.transpose(pt[:, j * 128:(j + 1) * 128], xb[:, fc * 128:(fc + 1) * 128], self.ident)
                return ins
            self.S.op("pe", fn, [xb_buf, self.cb], [pb])
            self.v("dve", "tensor_copy", [pb], [dst_buf], dstT[:, 4 * a:4 * a + 4, tok0:tok0 + 128],
                   pt.rearrange("p (j t) -> p j t", j=4))

    def phase_ln_in(self):
        S = self.S
        sem = self.sem("gb")
        gb = Buf("gb")
        G = self.load_bcast(sem, self.ln_in_g, gb)
        B = self.load_bcast(sem, self.ln_in_b, gb)
        zring = Ring(S, "z0", [self.alloc(D, F32) for _ in range(3)])
        xbs = Ring(S, "xb0", [self.alloc(D) for _ in range(2)])
        xts = Ring(S, "xt0", [self.alloc(16 * 512).rearrange("p (c t) -> p c t", c=16) for _ in range(1)])
        st = self.alloc(32, F32)
        mv = self.alloc(8, F32)
        tmpb = Buf("lntmp")

        def body(tb):
            xt, xtb, xtsem = xts.next()
            for tt in range(4):
                z, zb, zsem = zring.next()
                S.dma("sp", zsem, z, self.x_in[tb, tt * 128:(tt + 1) * 128, :], writes=[zb])
                xb, xbb, _ = xbs.next()
                self.layernorm_tile(z, zb, G, B, gb, xb, xbb, st, mv, tmpb)
                S.dma("pool", zsem, self.xres[tb, tt * 128:(tt + 1) * 128, :], z, reads=[zb], writes=[self.db["xres"]])
                self.transpose_tile(xb, xbb, xt, xtb, tt * 128)
            S.dma("pool", xtsem, self.xT[:, :, tb].rearrange("c p t -> p c t"), xt, reads=[xtb], writes=[self.db["xT"]])
        UNR = 4 if self.NC % 4 == 0 else (2 if self.NC % 2 == 0 else 1)

        def body_u(i):
            for k in range(UNR):
                body(i * UNR + k if UNR > 1 else i)
        self.loop(self.NC // UNR, body_u)
        self.phase_end()

    def wslab(self, ring, wsrc, wdb, k0, nk, n0, ncols):
        ap, buf, sem = ring.next()
        view = ap[:, 0:nk * ncols].rearrange("p (k n) -> p k n", k=nk)
        self.S.dma("sp", sem, view, wsrc[k0 * 128:(k0 + nk) * 128, n0:n0 + ncols].rearrange("(k p) n -> p k n", p=128),
                   reads=[wdb], writes=[buf])
        return view, buf

    def next_bank(self):
        bi = self.pbank_i % 8
        self.pbank_i += 1
        return bi

    def phase_p1(self, l):
        S = self.S
        W = self.wb_in[l]
        wdb = self.db[f"wb_in{l}"]
        wring = Ring(S, "p1w", [self.alloc(16 * 512) for _ in range(3)])
        xring = Ring(S, "p1x", [self.alloc(16 * 512).rearrange("p (c t) -> p c t", c=16) for _ in range(2)])
        oring = Ring(S, "p1o", [self.alloc(4 * 512) for _ in range(3)])
        vring = Ring(S, "p1v", [self.alloc(4 * NH * (VD + 1)) for _ in range(2)])
        qscale = HD ** -0.5
        qk16 = self.qkT.rearrange("a h j p c t -> (a h j) p c t")
        UNR = 4 if self.NC % 4 == 0 else (2 if self.NC % 2 == 0 else 1)

        def block(tb):
            evi = 0
            xt, xtb, xsem = xring.next()
            S.dma("sp", xsem, xt, self.xT[:, :, tb].rearrange("c p t -> p c t"), reads=[self.db["xT"]], writes=[xtb])
            for (off, ncols, dst, dname, cbase, scale) in ((OFF_Q, 2048, qk16, "qkT", 0, qscale),
                                                            (OFF_K, 2048, qk16, "qkT", 16, None),
                                                            (OFF_F, 1024, self.fT, "fT", 0, None)):
                for s0 in range(0, ncols, 512):
                    wv, wbuf = self.wslab(wring, W, wdb, 0, 16, off + s0, 512)
                    ot, ob, osem = oring.next()
                    otv = ot.rearrange("p (f t) -> p f t", f=4)
                    for fo in range(4):
                        bi = self.next_bank()
                        pb = self.pbuf[bi]
                        mms = [(self.bank(bi), wv[:, kc, fo * 128:(fo + 1) * 128], xt[:, kc, :], kc == 0, kc == 15)
                               for kc in range(16)]
                        self.mm_group(mms, [wbuf, xtb], [pb])
                        if scale is not None:
                            self.act(otv[:, fo, :], self.bank(bi), AF.Copy, [pb], [ob], scale=scale)
                        elif evi % 2 == 0:
                            self.v("dve", "tensor_copy", [pb], [ob], otv[:, fo, :], self.bank(bi))
                        else:
                            self.act(otv[:, fo, :], self.bank(bi), AF.Copy, [pb], [ob])
                        evi += 1
                    c0 = cbase + s0 // 128
                    S.dma("pool", osem, dst[c0:c0 + 4, :, tb].rearrange("c p t -> p c t"), otv,
                          reads=[ob], writes=[self.db[dname]])
            vt, vb, vsem = vring.next()
            vtv = vt.rearrange("p (t h d) -> p t h d", t=4, h=NH)
            self.v("pool", "memset", [], [vb], vtv[:, :, :, VD:VD + 1], 1.0)
            for s0 in range(0, 2048, 512):
                wv, wbuf = self.wslab(wring, W, wdb, 0, 16, OFF_V + s0, 512)
                for tt in range(4):
                    bi = self.next_bank()
                    pb = self.pbuf[bi]
                    mms = [(self.bank(bi), xt[:, kc, tt * 128:(tt + 1) * 128], wv[:, kc, :], kc == 0, kc == 15)
                           for kc in range(16)]
                    self.mm_group(mms, [wbuf, xtb], [pb])
                    h0 = s0 // VD
                    self.v("dve", "tensor_copy", [pb], [vb], vtv[:, tt, h0:h0 + 2, 0:VD],
                           self.bank(bi).rearrange("p (h d) -> p h d", h=2))
            S.dma("pool", vsem, self.vA[tb].rearrange("(t p) h d -> p t h d", p=128), vtv,
                  reads=[vb], writes=[self.db["vA"]])

        def body(i):
            for k in range(UNR):
                block(i * UNR + k if UNR > 1 else i)
        self.loop(self.NC // UNR, body)
        self.phase_end()

    def phase_attn(self, l):
        S = self.S
        T, NT, NC = self.T, self.NT, self.NC
        lam_init = 0.8 - 0.6 * math.exp(-0.3 * l)
        sem = self.sem("attc")
        cb2 = Buf("attc", persist=True)
        lamt = self.alloc(4 * HD, F32)
        S.dma("sp", sem, lamt, self.lam[l:l + 1, :].partition_broadcast(128), writes=[cb2])
        sc = self.alloc(16, F32)
        junk = self.alloc(HD, F32)
        self.v("dve", "tensor_tensor", [cb2], [cb2], junk, lamt[:, 0:HD], lamt[:, HD:2 * HD], ALU.mult)
        self.v("dve", "reduce_sum", [cb2], [cb2], sc[:, 0:1], junk, AX.X)
        self.v("dve", "tensor_tensor", [cb2], [cb2], junk, lamt[:, 2 * HD:3 * HD], lamt[:, 3 * HD:4 * HD], ALU.mult)
        self.v("dve", "reduce_sum", [cb2], [cb2], sc[:, 1:2], junk, AX.X)
        self.act(sc[:, 2:4], sc[:, 0:2], AF.Exp, [cb2], [cb2])
        self.v("dve", "scalar_tensor_tensor", [cb2], [cb2], sc[:, 4:5], sc[:, 3:4], -lam_init, sc[:, 2:3],
               ALU.add, ALU.subtract)
        neglam = sc[:, 4:5]
        gcol = self.alloc(2, F32)
        S.dma("sp", sem, gcol, self.subln_g[l:l + 1, :].rearrange("o (c p) -> p (o c)", p=128), writes=[cb2], slow=True)
        self.v("dve", "tensor_scalar", [cb2], [cb2], gcol, gcol, 1.0 - lam_init, None, ALU.mult)
        ncol = NT * NC
        maskt = self.alloc(ncol, F32)
        selm = self.alloc(ncol, F32)
        selp = self.alloc(ncol, F32)
        S.dma("sp", sem, maskt, self.c_mask, writes=[cb2])
        S.dma("sp", sem, selm, self.c_selm, writes=[cb2])
        S.dma("sp", sem, selp, self.c_selp, writes=[cb2])
        btab = self.alloc(ncol, F32)
        btabb = Buf("btab")
        relb = Ring(S, "relb", [self.alloc(2, F32)])
        bt_ring = Ring(S, "btr", [self.alloc(6 * 512)])
        kring = Ring(S, "kT", [self.alloc(2 * T)])
        vring = Ring(S, "vA", [self.alloc(NT * (VD + 1))])
        qring = Ring(S, "qT", [self.alloc(2 * 512) for _ in range(3)])
        pring = Ring(S, "pT", [self.alloc(512) for _ in range(6)])
        oring = Ring(S, "oacc", [self.alloc(2 * (VD + 1), F32) for _ in range(8)])
        ostage = Ring(S, "ostg", [self.alloc(2 * 512) for _ in range(2)])
        small = Ring(S, "osm", [self.alloc(16, F32) for _ in range(3)])
        otmp = Ring(S, "otmp", [self.alloc(VD, F32) for _ in range(3)])
        obf = Ring(S, "obf", [self.alloc(VD) for _ in range(3)])

        def body(h):
            rb, rbb, rbsem = relb.next()
            S.dma("sp", rbsem, rb[:, 0:1], self.rel_bias_t[h, 15:16, :].partition_broadcast(128), writes=[rbb])
            S.dma("sp", rbsem, rb[:, 1:2], self.rel_bias_t[h, 31:32, :].partition_broadcast(128), writes=[rbb])
            self.v("dve", "scalar_tensor_tensor", [cb2, rbb], [btabb], btab, selm, rb[:, 0:1], maskt, ALU.mult, ALU.add)
            self.v("dve", "scalar_tensor_tensor", [cb2, rbb, btabb], [btabb], btab, selp, rb[:, 1:2], btab,
                   ALU.mult, ALU.add)
            bt, btb, btsem = bt_ring.next()
            btv = bt.rearrange("p (r q) -> p r q", r=6)
            S.dma("sp", btsem, btv, self.btx[h].rearrange("r p q -> p r q"), reads=[self.db["btx"]], writes=[btb])
            kt_, kb, ksem = kring.next()
            ktv = kt_.rearrange("p (j t) -> p j t", j=2)
            S.dma("sp", ksem, ktv, self.qkT[1, h].rearrange("j p c t -> p j (c t)"), reads=[self.db["qkT"]], writes=[kb])
            va, vb, vsem = vring.next()
            vav = va.rearrange("p (t d) -> p t d", t=NT)
            for c8 in range(0, NC, 2):
                S.dma("sp", vsem, vav[:, c8 * 4:(c8 + 2) * 4, :],
                      self.vA[c8:c8 + 2, :, h, :].rearrange("c (t p) d -> p (c t) d", p=128),
                      reads=[self.db["vA"]], writes=[vb])
            pending = []
            for c in range(NC):
                qt, qb, qsem = qring.next()
                qtv = qt.rearrange("p (j t) -> p j t", j=2)
                S.dma("sp", qsem, qtv, self.qkT[0, h, :, :, c].rearrange("j p t -> p j t"),
                      reads=[self.db["qkT"]], writes=[qb])
                oacc_tiles = []
                for j in range(2):
                    pend = []

                    def emit_av(kt, pt, ptb, j=j):
                        mms = []
                        for qi in range(4):
                            mms.append((self.bank(qi)[:, 0:VD + 1], pt[:, qi * 128:(qi + 1) * 128], vav[:, kt, :],
                                        kt == 0, kt == NT - 1))
                        self.mm_group(mms, [ptb, vb], [self.pbuf[0], self.pbuf[1], self.pbuf[2], self.pbuf[3]])

                    for kt in range(NT):
                        r = kt - 4 * c
                        near = -1 <= r <= 4
                        bi = 4 + (self.pbank_i % 4)
                        self.pbank_i += 1
                        pb = self.pbuf[bi]
                        mms = [(self.bank(bi), ktv[:, j, kt * 128:(kt + 1) * 128], qtv[:, j, :], True, not near)]
                        rd = [kb, qb]
                        if near:
                            mms.append((self.bank(bi), self.anti, btv[:, r + 1, :], False, True))
                            rd += [btb, self.cb]
                        self.mm_group(mms, rd, [pb])
                        pt, ptb, _ = pring.next()
                        self.act(pt, self.bank(bi), AF.Exp, [pb, btabb], [ptb],
                                 bias=btab[:, kt * NC + c:kt * NC + c + 1], scale=1.0)
                        pend.append((kt, pt, ptb))
                        if len(pend) > 2:
                            emit_av(*pend.pop(0))
                    while pend:
                        emit_av(*pend.pop(0))
                    for qi in range(4):
                        if j == 0:
                            oacc_tiles.append(oring.next())
                        oa, oab, _ = oacc_tiles[qi]
                        oav = oa.rearrange("p (j d) -> p j d", j=2)
                        self.v("dve", "tensor_copy", [self.pbuf[qi]], [oab], oav[:, j, :], self.bank(qi)[:, 0:VD + 1])
                    if j == 0 and pending:
                        pending.pop(0)()
                def epilogue(c=c, oacc_tiles=oacc_tiles):
                    og, ogb, ogsem = ostage.next()
                    ogv = og.rearrange("p (c q) -> p c q", c=2)
                    for qi in range(4):
                        oa, oab, _ = oacc_tiles[qi]
                        oav = oa.rearrange("p (j d) -> p j d", j=2)
                        sm, smb, _ = small.next()
                        ot, otb, _ = otmp.next()
                        ob_, obb, _ = obf.next()
                        self.v("dve", "reciprocal", [oab], [smb], sm[:, 0:2], oav[:, :, VD])
                        self.v("dve", "tensor_tensor", [smb, cb2], [smb], sm[:, 1:2], sm[:, 1:2], neglam, ALU.mult)
                        self.v("dve", "tensor_scalar", [oab, smb], [otb], ot, oav[:, 0, 0:VD], sm[:, 0:1], None, ALU.mult)
                        self.v("dve", "scalar_tensor_tensor", [oab, smb, otb], [otb], ot, oav[:, 1, 0:VD], sm[:, 1:2], ot,
                               ALU.mult, ALU.add)
                        self.v("dve", "tensor_tensor", [otb], [oab], oav[:, 0, 0:VD], ot, ot, ALU.mult)
                        self.v("dve", "reduce_sum", [oab], [smb], sm[:, 2:3], oav[:, 0, 0:VD], AX.X)
                        self.v("dve", "tensor_scalar", [smb], [smb], sm[:, 3:4], sm[:, 2:3], 1.0 / VD, EPS, ALU.mult, ALU.add)
                        self.v("pool", "tensor_tensor", [smb, self.cb], [smb], sm[:, 4:5], sm[:, 3:4], self.neghalf, ALU.pow)
                        self.v("dve", "tensor_scalar", [otb, smb], [obb], ob_, ot, sm[:, 4:5], None, ALU.mult)
                        bi = 4 + (self.pbank_i % 4)
                        self.pbank_i += 1
                        pb = self.pbuf[bi]
                        ptp = self.bank(bi).bitcast(BF16)[:, 0:256]

                        def fn(eng, ptp=ptp, ob_=ob_):
                            eng.transpose(ptp[:, 0:128], ob_[:, 0:128], self.ident)
                            return eng.transpose(ptp[:, 128:256], ob_[:, 128:256], self.ident)
                        S.op("pe", fn, [obb, self.cb], [pb])
                        for cc in range(2):
                            self.v("dve", "tensor_scalar", [pb, cb2], [ogb], ogv[:, cc, qi * 128:(qi + 1) * 128],
                                   ptp[:, cc * 128:(cc + 1) * 128], gcol[:, cc:cc + 1], None, ALU.mult)
                    S.dma("pool", ogsem, self.oT[h, :, :, c].rearrange("j p t -> p j t"), ogv,
                          reads=[ogb], writes=[self.db["oT"]])
                pending.append(epilogue)
            while pending:
                pending.pop(0)()
        self.loop(NH, body)
        self.phase_end()

    def phase_fnet(self, l):
        S = self.S
        T, NT, NC = self.T, self.NT, self.NC
        GH = 4
        NB = 8 if NT >= 8 else NT
        UNR = 4 if NC % 4 == 0 else (2 if NC % 2 == 0 else 1)
        xcs = self.alloc(NT * GH * 256).rearrange("p (t g d) -> p t g d", t=NT, g=GH)
        xcsb = Buf("xcs", persist=True)
        fring = Ring(S, "fTl", [self.alloc(T) for _ in range(2)])
        dring = Ring(S, "dft", [self.alloc(NB * 2 * 512) for _ in range(2)])
        ystage = Ring(S, "ystg", [self.alloc(512) for _ in range(3)])
        for gh in range(NG // GH):
            for g in range(GH):
                ft, fb, fsem = fring.next()
                S.dma("sp", fsem, ft, self.fT[gh * GH + g].rearrange("p c t -> p (c t)"), reads=[self.db["fT"]],
                      writes=[fb])
                for t2 in range(0, NT, 2):
                    bi = self.next_bank()
                    pb = self.pbuf[bi]
                    mms = [(self.bank(bi)[:, k * 256:(k + 1) * 256], ft[:, (t2 + k) * 128:(t2 + k + 1) * 128], self.dftc,
                            True, True) for k in range(2)]
                    self.mm_group(mms, [fb, self.cb], [pb])
                    if (t2 // 2) % 2 == 0:
                        self.v("dve", "tensor_copy", [pb], [xcsb], xcs[:, t2:t2 + 2, g, :],
                               self.bank(bi).rearrange("p (k d) -> p k d", k=2))
                    else:
                        self.v("act", "copy", [pb], [xcsb], xcs[:, t2:t2 + 2, g, :],
                               self.bank(bi).rearrange("p (k d) -> p k d", k=2))

            def chunk(kc, gh=gh):
                banks = [self.next_bank() for _ in range(GH)]
                for n0 in range(0, NT, NB):
                    dt_, dbf, dsem = dring.next()
                    dtv = dt_.rearrange("p (n s k) -> p n s k", n=NB, s=2)
                    S.dma("sp", dsem, dtv, self.c_dft[kc, :, n0:n0 + NB], writes=[dbf])
                    mms = []
                    for nn in range(NB):
                        nt = n0 + nn
                        for g in range(GH):
                            mms.append((self.bank(banks[g]), xcs[:, nt, g, 0:128], dtv[:, nn, 0, :], nt == 0, False))
                            mms.append((self.bank(banks[g]), xcs[:, nt, g, 128:256], dtv[:, nn, 1, :], False,
                                        nt == NT - 1))
                    self.mm_group(mms, [dbf, xcsb], [self.pbuf[b] for b in banks])
                for g in range(GH):
                    ys, ysb, yssem = ystage.next()
                    if g % 2 == 0:
                        self.v("dve", "tensor_copy", [self.pbuf[banks[g]]], [ysb], ys, self.bank(banks[g]))
                    else:
                        self.v("act", "copy", [self.pbuf[banks[g]]], [ysb], ys, self.bank(banks[g]))
                    S.dma("pool", yssem, self.yT[gh * GH + g, :, kc], ys, reads=[ysb], writes=[self.db["yT"]])

            def body(i):
                for k in range(UNR):
                    chunk(i * UNR + k if UNR > 1 else i)
            self.loop(NC // UNR, body)
        self.phase_end()

    def gemm_tm_ln(self, aT, aTb, nk, W, wdb, wring, zt, ztb, kslab=16, ncols=512):
        kslabs = [(k0, min(kslab, nk - k0)) for k0 in range(0, nk, kslab)]
        for s0 in range(0, D, ncols):
            banks = [self.next_bank() for _ in range(4)]
            for si, (k0, kn) in enumerate(kslabs):
                wv, wbuf = self.wslab(wring, W, wdb, k0, kn, s0, ncols)
                for tt in range(4):
                    mms = [(self.bank(banks[tt])[:, 0:ncols], aT[:, k0 + kk, tt * 128:(tt + 1) * 128], wv[:, kk, :],
                            (k0 + kk) == 0, (k0 + kk) == nk - 1) for kk in range(kn)]
                    self.mm_group(mms, [wbuf, aTb], [self.pbuf[banks[tt]]])
            for tt in range(4):
                self.v("dve", "scalar_tensor_tensor", [self.pbuf[banks[tt]], ztb], [ztb],
                       zt[:, tt, s0:s0 + ncols], zt[:, tt, s0:s0 + ncols], ALPHA, self.bank(banks[tt])[:, 0:ncols],
                       ALU.mult, ALU.add)

    def phase_p4(self, l):
        S = self.S
        W_in = self.wb_in[l]
        sem = self.sem("gb")
        gb = Buf("gb4", persist=True)
        G = self.load_bcast(sem, self.ln1_g[l:l + 1, :], gb)
        B = self.load_bcast(sem, self.ln1_b[l:l + 1, :], gb)
        bg = self.alloc(32, F32)
        S.dma("sp", sem, bg, self.b_gate[l:l + 1, :].rearrange("o (c p) -> p (o c)", p=128), writes=[gb], slow=True)
        aring = Ring(S, "p4a", [self.alloc(40 * 512) for _ in range(1)])
        wring = Ring(S, "p4w", [self.alloc(16 * 256) for _ in range(6)])
        mT = self.alloc(16 * 512).rearrange("p (c t) -> p c t", c=16)
        mTb = Buf("mT")
        zring = Ring(S, "p4z", [self.alloc(4 * D, F32) for _ in range(1)])
        sg = Ring(S, "p4sg", [self.alloc(512, F32) for _ in range(4)])
        xbs = Ring(S, "p4xb", [self.alloc(D) for _ in range(2)])
        xts = Ring(S, "p4xt", [self.alloc(16 * 512) for _ in range(1)])
        st = self.alloc(32, F32)
        mv = self.alloc(8, F32)
        tmpb = Buf("lntmp4")
        oT16 = self.oT.rearrange("h j p c t -> (h j) p c t")

        def body(tb):
            a, ab, asem = aring.next()
            av = a.rearrange("p (c t) -> p c t", c=40)
            S.dma("sp", asem, av[:, 0:16, :], self.xT[:, :, tb].rearrange("c p t -> p c t"),
                  reads=[self.db["xT"]], writes=[ab])
            S.dma("sp", asem, av[:, 16:32, :], oT16[:, :, tb].rearrange("c p t -> p c t"),
                  reads=[self.db["oT"]], writes=[ab])
            S.dma("sp", asem, av[:, 32:40, :], self.yT[:, :, tb].rearrange("c p t -> p c t"),
                  reads=[self.db["yT"]], writes=[ab])
            zt_, ztb, zsem = zring.next()
            zt = zt_.rearrange("p (t d) -> p t d", t=4)
            S.dma("sp", zsem, zt, self.xres[tb].rearrange("(t p) d -> p t d", p=128),
                  reads=[self.db["xres"]], writes=[ztb])
            for s0 in range(0, D, 256):
                wga, wgab = self.wslab(wring, W_in, self.db[f"wb_in{l}"], 0, 16, OFF_G + s0, 256)
                wya, wyab = self.wslab(wring, self.wb_ba[l], self.db[f"wb_ba{l}"], 0, 16, s0, 256)
                wgf, wgfb = self.wslab(wring, W_in, self.db[f"wb_in{l}"], 0, 16, OFF_G + D + s0, 256)
                wyf, wyfb = self.wslab(wring, self.wb_bf[l], self.db[f"wb_bf{l}"], 0, 8, s0, 256)
                for fo in range(2):
                    fc = s0 // 128 + fo
                    fs = slice(fo * 128, (fo + 1) * 128)
                    bks = [self.next_bank() for _ in range(4)]
                    pbs = [self.pbuf[b] for b in bks]
                    self.mm_group([(self.bank(bks[0]), wga[:, kc, fs], av[:, kc, :], kc == 0, kc == 15) for kc in range(16)],
                                  [wgab, ab], [pbs[0]])
                    self.mm_group([(self.bank(bks[1]), wya[:, kc, fs], av[:, 16 + kc, :], kc == 0, kc == 15) for kc in range(16)],
                                  [wyab, ab], [pbs[1]])
                    self.mm_group([(self.bank(bks[2]), wgf[:, kc, fs], av[:, kc, :], kc == 0, kc == 15) for kc in range(16)],
                                  [wgfb, ab], [pbs[2]])
                    self.mm_group([(self.bank(bks[3]), wyf[:, kc, fs], av[:, 32 + kc, :], kc == 0, kc == 7) for kc in range(8)],
                                  [wyfb, ab], [pbs[3]])
                    sa, sab, _ = sg.next()
                    sf, sfb, _ = sg.next()
                    self.act(sa, self.bank(bks[0]), AF.Sigmoid, [pbs[0], gb], [sab], bias=bg[:, fc:fc + 1], scale=1.0)
                    self.act(sf, self.bank(bks[2]), AF.Sigmoid, [pbs[2], gb], [sfb], bias=bg[:, 16 + fc:16 + fc + 1], scale=1.0)
                    self.v("dve", "tensor_tensor", [sab, pbs[1]], [sab], sa, sa, self.bank(bks[1]), ALU.mult)
                    self.v("dve", "tensor_tensor", [sfb, pbs[3]], [sfb], sf, sf, self.bank(bks[3]), ALU.mult)
                    self.v("pool", "tensor_tensor", [sab, sfb], [mTb], mT[:, fc, :], sa, sf, ALU.add)
            self.gemm_tm_ln(mT, mTb, 16, self.wb_out[l], self.db[f"wb_out{l}"], wring, zt, ztb, ncols=256)
            xt_, xtb, xtsem = xts.next()
            xt = xt_.rearrange("p (c t) -> p c t", c=16)
            for tt in range(4):
                xb, xbb, _ = xbs.next()
                self.layernorm_tile(zt[:, tt, :], ztb, G, B, gb, xb, xbb, st, mv, tmpb)
                self.transpose_tile(xb, xbb, xt, xtb, tt * 128)
            S.dma("pool", zsem, self.x1res[tb].rearrange("(t p) d -> p t d", p=128), zt, reads=[ztb],
                  writes=[self.db["x1res"]])
            S.dma("pool", xtsem, self.x1T[:, :, tb].rearrange("c p t -> p c t"), xt, reads=[xtb],
                  writes=[self.db["x1T"]])
        UNR = 4 if self.NC % 4 == 0 else (2 if self.NC % 2 == 0 else 1)

        def body_u(i):
            for k in range(UNR):
                body(i * UNR + k if UNR > 1 else i)
        self.loop(self.NC // UNR, body_u)
        self.phase_end()

    def phase_p5(self, l, last):
        S = self.S
        sem = self.sem("gb")
        gb = Buf("gb5", persist=True)
        G = self.load_bcast(sem, self.ln2_g[l:l + 1, :], gb)
        B = self.load_bcast(sem, self.ln2_b[l:l + 1, :], gb)
        NKF = DFF // 128
        aring = Ring(S, "p5a", [self.alloc(16 * 512) for _ in range(1)])
        wring = Ring(S, "p5w", [self.alloc(16 * 512) for _ in range(3)])
        hT = self.alloc(NKF * 512).rearrange("p (c t) -> p c t", c=NKF)
        hTb = Buf("hT")
        zring = Ring(S, "p5z", [self.alloc(4 * D, F32) for _ in range(1)])
        sg = Ring(S, "p5sg", [self.alloc(512, F32) for _ in range(3)])
        xbs = Ring(S, "p5xb", [self.alloc(D) for _ in range(2)])
        xts = Ring(S, "p5xt", [self.alloc(16 * 512) for _ in range(1)])
        st = self.alloc(32, F32)
        mv = self.alloc(8, F32)
        tmpb = Buf("lntmp5")
        Wgu = self.wb_gu[l]
        dst = self.y_out if last else self.xres
        dstdb = self.db["y"] if last else self.db["xres"]

        def body(tb):
            a, ab, asem = aring.next()
            av = a.rearrange("p (c t) -> p c t", c=16)
            S.dma("sp", asem, av, self.x1T[:, :, tb].rearrange("c p t -> p c t"), reads=[self.db["x1T"]], writes=[ab])
            zt_, ztb, zsem = zring.next()
            zt = zt_.rearrange("p (t d) -> p t d", t=4)
            S.dma("sp", zsem, zt, self.x1res[tb].rearrange("(t p) d -> p t d", p=128),
                  reads=[self.db["x1res"]], writes=[ztb])
            for s0 in range(0, DFF, 512):
                wg, wgb = self.wslab(wring, Wgu, self.db[f"wb_gu{l}"], 0, 16, s0, 512)
                wu, wub = self.wslab(wring, Wgu, self.db[f"wb_gu{l}"], 0, 16, DFF + s0, 512)
                for fo in range(4):
                    fc = s0 // 128 + fo
                    fs = slice(fo * 128, (fo + 1) * 128)
                    b0, b1 = self.next_bank(), self.next_bank()
                    self.mm_group([(self.bank(b0), wg[:, kc, fs], av[:, kc, :], kc == 0, kc == 15) for kc in range(16)],
                                  [wgb, ab], [self.pbuf[b0]])
                    self.mm_group([(self.bank(b1), wu[:, kc, fs], av[:, kc, :], kc == 0, kc == 15) for kc in range(16)],
                                  [wub, ab], [self.pbuf[b1]])
                    s_, sb_, _ = sg.next()
                    self.act(s_, self.bank(b0), AF.Silu, [self.pbuf[b0]], [sb_])
                    self.v("dve", "tensor_tensor", [sb_, self.pbuf[b1]], [hTb], hT[:, fc, :], s_, self.bank(b1), ALU.mult)
            self.gemm_tm_ln(hT, hTb, NKF, self.wb_dn[l], self.db[f"wb_dn{l}"], wring, zt, ztb, kslab=16)
            xt_, xtb, xtsem = xts.next()
            xt = xt_.rearrange("p (c t) -> p c t", c=16)
            for tt in range(4):
                xb, xbb, _ = xbs.next()
                self.layernorm_tile(zt[:, tt, :], ztb, G, B, gb, xb, xbb, st, mv, tmpb)
                if not last:
                    self.transpose_tile(xb, xbb, xt, xtb, tt * 128)
            S.dma("pool", zsem, dst[tb].rearrange("(t p) d -> p t d", p=128), zt, reads=[ztb], writes=[dstdb])
            if not last:
                S.dma("pool", xtsem, self.xT[:, :, tb].rearrange("c p t -> p c t"), xt, reads=[xtb],
                      writes=[self.db["xT"]])
        UNR = 4 if self.NC % 4 == 0 else (2 if self.NC % 2 == 0 else 1)

        def body_u(i):
            for k in range(UNR):
                body(i * UNR + k if UNR > 1 else i)
        self.loop(self.NC // UNR, body_u)
        self.phase_end()


def _rel_bucket_np(rel):
    nb = 16
    ret = np.where(rel > 0, nb, 0)
    n = np.abs(rel)
    max_exact = nb // 2
    nf = np.maximum(n, 1).astype(np.float32)
    large = max_exact + (np.log(nf / np.float32(max_exact)) / np.float32(math.log(128 / max_exact))
                         * np.float32(nb - max_exact)).astype(np.int32)
    large = np.minimum(large, nb - 1)
    return ret + np.where(n < max_exact, n, large)


def host_consts(T, seq_len):
    NT, NC = T // 128, T // 512
    bf = ml_dtypes.bfloat16
    n = np.arange(T)
    seg = n // seq_len
    dft = np.zeros((NC, 128, NT, 2, 512), dtype=bf)
    pos = (n % seq_len).astype(np.int64)
    for kc in range(NC):
        k = np.arange(kc * 512, (kc + 1) * 512)
        same = (seg[:, None] == seg[None, k])
        ang = 2.0 * np.pi * ((pos[:, None] * pos[None, k]) % seq_len) / seq_len
        cm = np.where(same, np.cos(ang), 0.0) / math.sqrt(seq_len)
        sm = np.where(same, np.sin(ang), 0.0) / math.sqrt(seq_len)
        dft[kc, :, :, 0, :] = cm.reshape(NT, 128, 512).transpose(1, 0, 2).astype(bf)
        dft[kc, :, :, 1, :] = sm.reshape(NT, 128, 512).transpose(1, 0, 2).astype(bf)
    cc = np.arange(128)
    angc = 2.0 * np.pi * ((cc[:, None] * cc[None, :]) % 128) / 128
    dftc = np.concatenate([np.cos(angc), -np.sin(angc)], axis=1) / math.sqrt(128)
    kt = np.arange(NT)[:, None]
    c = np.arange(NC)[None, :]
    same_seq = (kt * 128) // seq_len == (c * 512) // seq_len
    mask = np.where(same_seq, 0.0, MASKV)
    selm = (kt < 4 * c - 1).astype(np.float32)
    selp = (kt > 4 * c + 4).astype(np.float32)
    rep = lambda a: np.ascontiguousarray(np.broadcast_to(a.reshape(1, -1).astype(np.float32), (128, NT * NC)))
    u = np.arange(TABW)
    bucket = _rel_bucket_np((639 - u).astype(np.int32))
    oh = np.zeros((32, TABW), np.float32)
    oh[bucket, u] = 1.0
    ident = np.eye(128, dtype=np.float32)
    return {
        "c_dft": dft, "c_dftc": dftc.astype(bf), "c_mask": rep(mask), "c_selm": rep(selm), "c_selp": rep(selp),
        "c_oh": oh, "c_ident": ident.astype(bf), "c_anti": ident[::-1].copy().astype(bf),
    }


_PROG_CACHE = {}


def get_prog(T, nlayers=DEPTH, debug=False):
    key = (T, nlayers, debug)
    if key not in _PROG_CACHE:
        p = Prog(T, nlayers, debug=debug)
        p.build()
        _PROG_CACHE[key] = p
    return _PROG_CACHE[key]


def make_in_map(x_core, seq_len, params, consts_cache):
    T = x_core.shape[0]
    ck = (T, seq_len)
    if ck not in consts_cache:
        consts_cache[ck] = host_consts(T, seq_len)
    m = {"x": np.ascontiguousarray(x_core, dtype=np.float32).reshape(T // 512, 512, D)}
    m.update(params)
    m.update(consts_cache[ck])
    return m


def prep_params(rel_bias, ln_in_g, ln_in_b, w_in, b_gate, lam, subln_g, w_br_attn, w_br_fnet, w_out,
                ln1_g, ln1_b, w_gu, w_down, ln2_g, ln2_b):
    f = lambda a: np.ascontiguousarray(np.asarray(a, dtype=np.float32))
    nl = np.asarray(w_in).shape[0]
    return {
        "rel_bias": f(rel_bias), "rel_bias_t": f(np.asarray(rel_bias).T).reshape(NH, 32, 1), "ln_in_g": f(ln_in_g).reshape(1, D), "ln_in_b": f(ln_in_b).reshape(1, D),
        "w_in": f(w_in), "b_gate": f(b_gate), "lam": f(lam).reshape(nl, 4 * HD), "subln_g": f(subln_g),
        "w_br_attn": f(w_br_attn), "w_br_fnet": f(w_br_fnet), "w_out": f(w_out),
        "ln1_g": f(ln1_g), "ln1_b": f(ln1_b), "w_gu": f(w_gu), "w_down": f(w_down),
        "ln2_g": f(ln2_g), "ln2_b": f(ln2_b),
    }


def kernel(x_prompt, x_sample, **params):
    x_prompt = np.asarray(x_prompt)
    x_sample = np.asarray(x_sample)
    P = prep_params(**params)
    T = 8192
    prog = get_prog(T)
    cache = {}
    in_maps = []
    for i in range(4):
        in_maps.append(make_in_map(x_sample[i], 8192, P, cache))
    for i in range(4):
        in_maps.append(make_in_map(x_prompt[4 * i:4 * i + 4].reshape(T, D), 2048, P, cache))
    res = run_bass_kernel_spmd(prog.nc, in_maps, core_ids=list(range(8)))
    ys = [np.asarray(r["y"], dtype=np.float32).reshape(T, D) for r in res.results]
    y_sample = np.stack(ys[0:4], axis=0)
    y_prompt = np.concatenate([y.reshape(4, 2048, D) for y in ys[4:8]], axis=0)
    return (y_prompt, y_sample)
```
